# Optimizing a Trainium2 kernel written in Bass

```python
import math
import jax
import jax.numpy as jnp
from jax import lax
import numpy as np

D_MODEL = 4096
BATCH = 2
SEQ = 4096
DEPTH = 2

N_META = 16
CHUNK = 64
N_BRANCH = 4
MIX_W = D_MODEL // N_BRANCH
EPS = 1e-6

RET_HEADS = 4
RET_DK = MIX_W // RET_HEADS
RET_DV = MIX_W // RET_HEADS
ROPE_BASE = 10000.0
GLA_HEADS = 4
GLA_DK = MIX_W // (2 * GLA_HEADS)
GLA_DV = MIX_W // GLA_HEADS
GLA_RANK = 16
GLA_TAU = 16.0
GDN_HEADS = 8
GDN_DK = MIX_W // GDN_HEADS
GDN_DV = MIX_W // GDN_HEADS
GDN_CONV = 4
RWKV_HEAD = 64
RWKV_HEADS = MIX_W // RWKV_HEAD
RWKV_W_RANK = 64
RWKV_A_RANK = 64
RWKV_G_RANK = 160
RWKV_DECAY_SCALE = 0.606531
RWKV_GN_EPS = 64e-5
RWKV_SRC_W = 3 * MIX_W + RWKV_W_RANK + RWKV_A_RANK + RWKV_G_RANK
RWKV_SPLIT_IDX = tuple(int(i) for i in np.cumsum((MIX_W, MIX_W, MIX_W, RWKV_W_RANK, RWKV_A_RANK, RWKV_G_RANK))[:-1])
D_FF = 11008
FFN_CONV = 3

SPLIT_SIZES = (
    RET_HEADS * RET_DK, RET_HEADS * RET_DK, RET_HEADS * RET_DV, RET_HEADS * RET_DV,
    GLA_HEADS * GLA_DK, GLA_HEADS * GLA_DK, GLA_HEADS * GLA_DV, GLA_HEADS * GLA_DV, GLA_RANK,
    GDN_HEADS * GDN_DK, GDN_HEADS * GDN_DK, GDN_HEADS * GDN_DV, GDN_HEADS * GDN_DV,
    GDN_HEADS, GDN_HEADS,
    RWKV_SRC_W,
)
IN_W = sum(SPLIT_SIZES)
SPLIT_IDX = tuple(int(i) for i in np.cumsum(SPLIT_SIZES)[:-1])
F32 = jnp.float32

kernel_name = 'hybrid_gated_parallel_mixer_trunk'


def _rmsnorm(x, gain):
    xf = x.astype(F32)
    y = xf * lax.rsqrt(jnp.mean(xf * xf, axis=-1, keepdims=True) + EPS)
    return (y * gain.astype(F32)).astype(x.dtype)


def _head_norm(t, gain, eps=EPS, center=False):
    if center:
        t = t - jnp.mean(t, axis=-1, keepdims=True)
    return t * lax.rsqrt(jnp.mean(t * t, axis=-1, keepdims=True) + eps) * gain.astype(F32)


def _l2norm(t):
    return t * lax.rsqrt(jnp.sum(t * t, axis=-1, keepdims=True) + EPS)


def _heads(t, n_heads):
    return t.reshape(t.shape[:-1] + (n_heads, t.shape[-1] // n_heads))


def _token_shift(t):
    return jnp.pad(t, ((0, 0), (1, 0), (0, 0)))[:, :-1]


def _causal_dwconv(t, w):
    K, C = w.shape
    return lax.conv_general_dilated(t, w[:, None, :].astype(t.dtype), window_strides=(1,),
                                    padding=((K - 1, 0),), dimension_numbers=('NWC', 'WIO', 'NWC'),
                                    feature_group_count=C)


def _rope(t, pos):
    half = t.shape[-1] // 2
    inv_freq = ROPE_BASE ** (-jnp.arange(half, dtype=F32) / half)
    ang = pos.astype(F32)[:, None] * inv_freq[None, :]
    cos = jnp.cos(ang)[None, :, None, :]
    sin = jnp.sin(ang)[None, :, None, :]
    t1, t2 = t[..., :half], t[..., half:]
    return jnp.concatenate([t1 * cos - t2 * sin, t1 * sin + t2 * cos], axis=-1)


def _to_chunks(t):
    t = jnp.pad(t, ((0, 0), (CHUNK - N_META, 0), (0, 0), (0, 0)))
    B, T, H, d = t.shape
    return jnp.transpose(t.reshape(B, T // CHUNK, CHUNK, H, d), (1, 0, 3, 2, 4))


def _from_chunks(o):
    N, B, H, C, d = o.shape
    o = jnp.transpose(o, (1, 0, 3, 2, 4)).reshape(B, N * C, H, d)
    return o[:, CHUNK - N_META:]


def _retention_chunked(q, k, v, gamma):
    log_g = jnp.log(gamma)
    idx = jnp.arange(CHUNK, dtype=F32)
    rel = idx[:, None] - idx[None, :]
    decay = jnp.where(rel >= 0, jnp.exp(log_g[:, None, None] * jnp.maximum(rel, 0.0)), 0.0)
    q_decay = jnp.exp(log_g[:, None] * (idx + 1.0))[:, :, None]
    k_decay = jnp.exp(log_g[:, None] * (CHUNK - 1.0 - idx))[:, :, None]
    s_decay = jnp.exp(log_g * CHUNK)[:, None, None]

    def step(S, inp):
        qi, ki, vi = inp
        scores = jnp.einsum('bhid,bhjd->bhij', qi, ki) * decay
        o = jnp.einsum('bhij,bhjv->bhiv', scores, vi) + jnp.einsum('bhid,bhdv->bhiv', qi, S) * q_decay
        S = S * s_decay + jnp.einsum('bhjd,bhjv->bhdv', ki * k_decay, vi)
        return S, o

    B, _, H, dk = q.shape
    S0 = jnp.zeros((B, H, dk, v.shape[-1]), F32)
    _, o = lax.scan(step, S0, (_to_chunks(q), _to_chunks(k), _to_chunks(v)))
    return _from_chunks(o)


def _gla_chunked(q, k, v, log_a):
    B, _, H, dk = q.shape
    incl = jnp.tril(jnp.ones((CHUNK, CHUNK), bool))[:, :, None]

    def step(S, inp):
        qi, ki, vi, li = inp
        b = jnp.cumsum(li, axis=-2)
        diff = jnp.where(incl, b[:, :, :, None, :] - b[:, :, None, :, :], -jnp.inf)
        scores = jnp.einsum('bhid,bhjd,bhijd->bhij', qi, ki, jnp.exp(diff))
        o = jnp.einsum('bhij,bhjv->bhiv', scores, vi) + jnp.einsum('bhid,bhdv->bhiv', qi * jnp.exp(b), S)
        b_last = b[:, :, -1:, :]
        S = S * jnp.exp(b_last)[:, :, 0, :, None] + jnp.einsum('bhjd,bhjv->bhdv', ki * jnp.exp(b_last - b), vi)
        return S, o

    S0 = jnp.zeros((B, H, dk, v.shape[-1]), F32)
    _, o = lax.scan(step, S0, (_to_chunks(q), _to_chunks(k), _to_chunks(v), _to_chunks(log_a)))
    return _from_chunks(o)


def _gated_delta_chunked(q, k, v, g, beta):
    B, _, H, dk = q.shape
    dv = v.shape[-1]
    incl = jnp.tril(jnp.ones((CHUNK, CHUNK), bool))
    strict = jnp.tril(jnp.ones((CHUNK, CHUNK), bool), -1)
    eye = jnp.eye(CHUNK, dtype=F32)

    def step(S, inp):
        qi, ki, vi, gi, bi = inp
        gcum = jnp.cumsum(gi, axis=-1)
        decay = jnp.exp(jnp.where(incl, gcum[..., :, None] - gcum[..., None, :], -jnp.inf))
        kb = ki * bi[..., None]
        a_mat = eye + jnp.where(strict, jnp.einsum('bhid,bhjd->bhij', kb, ki) * decay, 0.0)
        rhs = jnp.concatenate([vi * bi[..., None], kb * jnp.exp(gcum)[..., None]], axis=-1)
        sol = lax.linalg.triangular_solve(a_mat, rhs, left_side=True, lower=True, unit_diagonal=True)
        u, w = sol[..., :dv], sol[..., dv:]
        v_new = u - jnp.einsum('bhcd,bhdv->bhcv', w, S)
        attn = jnp.einsum('bhid,bhjd->bhij', qi, ki) * decay
        o = jnp.einsum('bhid,bhdv->bhiv', qi * jnp.exp(gcum)[..., None], S) + jnp.einsum('bhij,bhjv->bhiv', attn, v_new)
        g_last = gcum[..., -1:]
        S = S * jnp.exp(g_last)[..., None] + jnp.einsum('bhjd,bhjv->bhdv', ki * jnp.exp(g_last - gcum)[..., None], v_new)
        return S, o

    S0 = jnp.zeros((B, H, dk, dv), F32)
    xs = (_to_chunks(q), _to_chunks(k), _to_chunks(v), _to_chunks(g[..., None])[..., 0], _to_chunks(beta[..., None])[..., 0])
    _, o = lax.scan(step, S0, xs)
    return _from_chunks(o)


def _rwkv7_scan(r, decay, k, v, kk, a):
    B, L, H, N = r.shape

    def step(S, inp):
        rt, wt, kt, vt, kkt, at = inp
        sa = jnp.einsum('bhvk,bhk->bhv', S, -kkt)
        S = S * wt[:, :, None, :] + sa[..., None] * (kkt * at)[:, :, None, :] + vt[..., None] * kt[:, :, None, :]
        return S, jnp.einsum('bhvk,bhk->bhv', S, rt)

    xs = tuple(jnp.moveaxis(t, 1, 0) for t in (r, decay, k, v, kk, a))
    _, y = lax.scan(step, jnp.zeros((B, H, N, N), F32), xs)
    return jnp.moveaxis(y, 0, 1)


def _retention_branch(q, k, v, gate, norm_gain):
    B, L, _ = q.shape
    pos = jnp.arange(L)
    q = _rope(_heads(q.astype(F32), RET_HEADS), pos) * RET_DK ** -0.5
    k = _rope(_heads(k.astype(F32), RET_HEADS), pos)
    v = _heads(v.astype(F32), RET_HEADS)
    gamma = 1.0 - 2.0 ** (-5.0 - jnp.arange(RET_HEADS, dtype=F32))
    o = _retention_chunked(q, k, v, gamma)
    o = _head_norm(o, norm_gain.reshape(RET_HEADS, RET_DV), center=True)
    return (o.reshape(B, L, MIX_W) * jax.nn.silu(gate.astype(F32))).astype(gate.dtype)


def _gla_branch(q, k, v, gate, lr, w2, bias, norm_gain):
    B, L, _ = q.shape
    q = _heads(q.astype(F32), GLA_HEADS) * GLA_DK ** -0.5
    k = _heads(k.astype(F32), GLA_HEADS)
    v = _heads(v.astype(F32), GLA_HEADS)
    log_a = _heads(jax.nn.log_sigmoid(lr.astype(F32) @ w2 + bias) / GLA_TAU, GLA_HEADS)
    o = _gla_chunked(q, k, v, log_a)
    o = _head_norm(o, norm_gain.reshape(GLA_HEADS, GLA_DV))
    return (o.reshape(B, L, MIX_W) * jax.nn.silu(gate.astype(F32))).astype(gate.dtype)


def _gdn_branch(q, k, v, z, a, b, conv_w, a_log, dt_bias, norm_gain):
    B, L, _ = q.shape
    qkv = jax.nn.silu(_causal_dwconv(jnp.concatenate([q, k, v], axis=-1).astype(F32), conv_w.astype(F32)))
    q, k, v = jnp.split(qkv, 3, axis=-1)
    q = _l2norm(_heads(q, GDN_HEADS)) * GDN_DK ** -0.5
    k = _l2norm(_heads(k, GDN_HEADS))
    v = _heads(v, GDN_HEADS)
    g = -jnp.exp(a_log.astype(F32)) * jax.nn.softplus(a.astype(F32) + dt_bias)
    beta = jax.nn.sigmoid(b.astype(F32))
    o = _gated_delta_chunked(q, k, v, g, beta)
    o = _head_norm(o, norm_gain) * jax.nn.silu(_heads(z.astype(F32), GDN_HEADS))
    return o.reshape(B, L, MIX_W).astype(z.dtype)


def _rwkv7_branch(src, mu, w0, w2, a0, a2, g2, k_k, k_a, r_k, ln_w, ln_b):
    B, L, _ = src.shape
    s = src.astype(F32)
    s = s + (_token_shift(s) - s) * mu.astype(F32)
    r, k, v, w_src, a_src, g_src = jnp.split(s, RWKV_SPLIT_IDX, axis=-1)
    decay = jnp.exp(-RWKV_DECAY_SCALE * jax.nn.sigmoid(w0 + jnp.tanh(w_src) @ w2))
    a = jax.nn.sigmoid(a0 + a_src @ a2)
    g = jax.nn.sigmoid(g_src) @ g2
    kk = _l2norm(_heads(k * k_k, RWKV_HEADS))
    k = k * (1.0 + (a - 1.0) * k_a)
    r, k, v, decay, a = (_heads(t, RWKV_HEADS) for t in (r, k, v, decay, a))
    y = _rwkv7_scan(r, decay, k, v, kk, a)
    y = _head_norm(y, ln_w.reshape(RWKV_HEADS, RWKV_HEAD), eps=RWKV_GN_EPS, center=True) + ln_b.reshape(RWKV_HEADS, RWKV_HEAD)
    y = y + jnp.sum(r * k * r_k, axis=-1, keepdims=True) * v
    return (y.reshape(B, L, MIX_W) * g).astype(src.dtype)


def _mixer_sublayer(x, pre_g, w_in, ret_norm, gla_w2, gla_b, gla_norm, gdn_conv, gdn_a_log, gdn_dt_bias,
                    gdn_norm, rwkv_mu, rwkv_w0, rwkv_w2, rwkv_a0, rwkv_a2, rwkv_g2, rwkv_kk, rwkv_ka, rwkv_rk,
                    rwkv_ln_w, rwkv_ln_b, w_branch, w_gate, w_out, post_g):
    B, L, D = x.shape
    h = _rmsnorm(x, pre_g)
    (qa, ka, va, ga, qb, kb, vb, gb, lb, qc, kc, vc, zc, ac, bc, src_d) = jnp.split(h @ w_in, SPLIT_IDX, axis=-1)
    o_a = _retention_branch(qa, ka, va, ga, ret_norm)
    o_b = _gla_branch(qb, kb, vb, gb, lb, gla_w2, gla_b, gla_norm)
    o_c = _gdn_branch(qc, kc, vc, zc, ac, bc, gdn_conv, gdn_a_log, gdn_dt_bias, gdn_norm)
    o_d = _rwkv7_branch(src_d, rwkv_mu, rwkv_w0, rwkv_w2, rwkv_a0, rwkv_a2, rwkv_g2, rwkv_kk, rwkv_ka, rwkv_rk,
                        rwkv_ln_w, rwkv_ln_b)
    o = jnp.stack([o_a, o_b, o_c, o_d], axis=2)
    y = jnp.einsum('blnc,ncd->blnd', o, w_branch)
    gate = jax.nn.sigmoid((h @ w_gate).reshape(B, L, N_BRANCH, D))
    merged = jnp.sum(gate * y, axis=2)
    return x + _rmsnorm(merged @ w_out, post_g)


def _ffn_sublayer(x, pre_g, w_up, conv_w, w_down, post_g):
    h = _rmsnorm(x, pre_g)
    u = _causal_dwconv(h @ w_up, conv_w)
    a, b = jnp.split(u, 2, axis=-1)
    return x + _rmsnorm((jax.nn.silu(a) * b) @ w_down, post_g)


def setup_inputs(seed: int = 0) -> dict:
    key = jax.random.key(seed)
    ks = iter(jax.random.split(key, 48))

    def nrm(shape, scale):
        return jax.random.normal(next(ks), shape, F32) * scale

    def unif(shape, lo, hi):
        return jax.random.uniform(next(ks), shape, F32, lo, hi)

    def gain(shape):
        return 1.0 + nrm(shape, 0.02)

    dt = jnp.exp(unif((DEPTH, GDN_HEADS), math.log(1e-3), math.log(1e-1)))
    return {
        'x': nrm((BATCH, SEQ, D_MODEL), 1.0),
        'meta': nrm((N_META, D_MODEL), 1.0),
        'pre_mix': gain((DEPTH, D_MODEL)),
        'w_in': nrm((DEPTH, D_MODEL, IN_W), D_MODEL ** -0.5),
        'ret_norm': gain((DEPTH, MIX_W)),
        'gla_w2': nrm((DEPTH, GLA_RANK, GLA_HEADS * GLA_DK), GLA_RANK ** -0.5),
        'gla_b': nrm((DEPTH, GLA_HEADS * GLA_DK), 0.01),
        'gla_norm': gain((DEPTH, MIX_W)),
        'gdn_conv': nrm((DEPTH, GDN_CONV, 3 * MIX_W), GDN_CONV ** -0.5),
        'gdn_a_log': jnp.log(unif((DEPTH, GDN_HEADS), 1.0, 16.0)),
        'gdn_dt_bias': dt + jnp.log(-jnp.expm1(-dt)),
        'gdn_norm': gain((DEPTH, GDN_DV)),
        'rwkv_mu': unif((DEPTH, RWKV_SRC_W), 0.0, 1.0),
        'rwkv_w0': nrm((DEPTH, MIX_W), 1.0),
        'rwkv_w2': nrm((DEPTH, RWKV_W_RANK, MIX_W), 0.1 * RWKV_W_RANK ** -0.5),
        'rwkv_a0': nrm((DEPTH, MIX_W), 0.1),
        'rwkv_a2': nrm((DEPTH, RWKV_A_RANK, MIX_W), 0.1 * RWKV_A_RANK ** -0.5),
        'rwkv_g2': nrm((DEPTH, RWKV_G_RANK, MIX_W), RWKV_G_RANK ** -0.5),
        'rwkv_kk': 0.85 + nrm((DEPTH, MIX_W), 0.02),
        'rwkv_ka': 1.0 + nrm((DEPTH, MIX_W), 0.02),
        'rwkv_rk': nrm((DEPTH, RWKV_HEADS, RWKV_HEAD), 0.1),
        'rwkv_ln_w': gain((DEPTH, MIX_W)),
        'rwkv_ln_b': nrm((DEPTH, MIX_W), 0.01),
        'w_branch': nrm((DEPTH, N_BRANCH, MIX_W, D_MODEL), MIX_W ** -0.5),
        'w_gate': nrm((DEPTH, D_MODEL, N_BRANCH * D_MODEL), D_MODEL ** -0.5),
        'w_out': nrm((DEPTH, D_MODEL, D_MODEL), D_MODEL ** -0.5),
        'post_mix': gain((DEPTH, D_MODEL)),
        'pre_ffn': gain((DEPTH, D_MODEL)),
        'w_up': nrm((DEPTH, D_MODEL, 2 * D_FF), D_MODEL ** -0.5),
        'ffn_conv': nrm((DEPTH, FFN_CONV, 2 * D_FF), FFN_CONV ** -0.5),
        'w_down': nrm((DEPTH, D_FF, D_MODEL), D_FF ** -0.5),
        'post_ffn': gain((DEPTH, D_MODEL)),
    }


def reference(x, meta, pre_mix, w_in, ret_norm, gla_w2, gla_b, gla_norm, gdn_conv, gdn_a_log, gdn_dt_bias,
              gdn_norm, rwkv_mu, rwkv_w0, rwkv_w2, rwkv_a0, rwkv_a2, rwkv_g2, rwkv_kk, rwkv_ka, rwkv_rk,
              rwkv_ln_w, rwkv_ln_b, w_branch, w_gate, w_out, post_mix, pre_ffn, w_up, ffn_conv, w_down, post_ffn):
    B = x.shape[0]
    h = jnp.concatenate([jnp.broadcast_to(meta.astype(x.dtype)[None], (B, N_META, x.shape[-1])), x], axis=1)
    for i in range(DEPTH):
        h = _mixer_sublayer(h, pre_mix[i], w_in[i], ret_norm[i], gla_w2[i], gla_b[i], gla_norm[i], gdn_conv[i],
                            gdn_a_log[i], gdn_dt_bias[i], gdn_norm[i], rwkv_mu[i], rwkv_w0[i], rwkv_w2[i],
                            rwkv_a0[i], rwkv_a2[i], rwkv_g2[i], rwkv_kk[i], rwkv_ka[i], rwkv_rk[i], rwkv_ln_w[i],
                            rwkv_ln_b[i], w_branch[i], w_gate[i], w_out[i], post_mix[i])
        h = _ffn_sublayer(h, pre_ffn[i], w_up[i], ffn_conv[i], w_down[i], post_ffn[i])
    return h[:, N_META:]
```

```python
from contextlib import ExitStack
import numpy as np
import concourse.bass as bass
import concourse.mybir as mybir
from concourse.bass_utils import run_bass_kernel_spmd

F32 = mybir.dt.float32
ALU = mybir.AluOpType
AF = mybir.ActivationFunctionType
AX = mybir.AxisListType

D = 4096
DFF = 11008
NMETA = 16
EPS = 1e-6
NCORES = 8


class Obj:
    def __init__(self, t, name):
        self.t = t
        self.name = name
        self.w = {}
        self.r = {}

    def __getitem__(self, k):
        return self.t[k]


class Prog:
    COMPUTE = ("scalar", "vector", "gpsimd", "tensor")
    NDS = 12

    def __init__(self, nc, stack):
        self.nc = nc
        self.stack = stack
        self.ops = {e: [] for e in self.COMPUTE + ("sync",)}
        self.semh = {}
        self.cnt = {}
        for e in self.COMPUTE:
            self.semh["s_" + e] = stack.enter_context(nc.semaphore("s_" + e))
            self.cnt[e] = 0
        self.dcnt = [0] * self.NDS
        self.dnext = 0
        for j in range(self.NDS):
            self.semh[f"d{j}"] = stack.enter_context(nc.semaphore(f"d{j}"))
        self.seen = {e: {} for e in self.ops}
        self.nuid = 0

    def uid(self, p):
        self.nuid += 1
        return f"{p}{self.nuid}"

    def sb(self, shape, name=None, dtype=F32):
        name = name or self.uid("t")
        t = self.stack.enter_context(self.nc.sbuf_tensor(name, list(shape), dtype))
        return Obj(t, name)

    def ps(self, name=None):
        name = name or self.uid("p")
        t = self.stack.enter_context(self.nc.psum_tensor(name, [128, 512], F32))
        return Obj(t, name)

    def dram(self, shape, name=None):
        name = name or self.uid("dr")
        t = self.nc.dram_tensor(name, list(shape), F32, kind="Internal")
        return Obj(t.ap(), name)

    def _deps(self, reads, writes):
        waits = {}
        for t in reads:
            for k, v in t.w.items():
                waits[k] = max(waits.get(k, 0), v)
        for t in writes:
            for k, v in list(t.w.items()) + list(t.r.items()):
                waits[k] = max(waits.get(k, 0), v)
        return waits

    def _commit(self, eng, waits, fn, key, inc, val, reads, writes):
        seen = self.seen[eng]
        wl = []
        for k, v in waits.items():
            if seen.get(k, 0) < v:
                seen[k] = v
                wl.append((k, v))
        self.ops[eng].append((wl, fn, key, inc))
        for t in writes:
            t.w = {key: val}
            t.r = {}
        for t in reads:
            if t not in writes:
                t.r[key] = max(t.r.get(key, 0), val)

    def issue(self, eng, fn, reads=(), writes=()):
        waits = self._deps(reads, writes)
        key = "s_" + eng
        if eng == "tensor":
            waits.pop(key, None)
        self.cnt[eng] += 1
        self._commit(eng, waits, fn, key, 1, self.cnt[eng], reads, writes)

    def dma(self, out_ap, in_ap, reads=(), writes=(), queue="sync"):
        j = self.dnext
        self.dnext = (j + 1) % self.NDS
        key = f"d{j}"
        waits = self._deps(reads, writes)
        if self.dcnt[j] > 0:
            waits[key] = max(waits.get(key, 0), self.dcnt[j])
        self.dcnt[j] += 16
        self._commit(queue, waits, lambda e, o=out_ap, i=in_ap: e.dma_start(out=o, in_=i),
                     key, 16, self.dcnt[j], reads, writes)

    def emit(self):
        nc = self.nc
        finals = [(f"d{j}", self.dcnt[j]) for j in range(self.NDS) if self.dcnt[j] > 0]

        def run(e, name):
            for wl, fn, key, inc in self.ops[name]:
                for k, v in wl:
                    e.wait_ge(self.semh[k], v)
                fn(e).then_inc(self.semh[key], inc)
            if name == "sync":
                for k, v in finals:
                    e.wait_ge(self.semh[k], v)

        with nc.Block() as block:
            @block.sync
            def _(e):
                run(e, "sync")

            @block.scalar
            def _(e):
                run(e, "scalar")

            @block.vector
            def _(e):
                run(e, "vector")

            @block.gpsimd
            def _(e):
                run(e, "gpsimd")

            @block.tensor
            def _(e):
                run(e, "tensor")


def mm(P, out, out_ap, lhsT, lhsT_ap, rhs, rhs_ap, start, stop):
    P.issue("tensor", lambda e: e.matmul(out_ap, lhsT_ap, rhs_ap, start=start, stop=stop),
            reads=(lhsT, rhs), writes=(out,))


def act(P, out, out_ap, in_, in_ap, func, scale=1.0, bias=None, extra_reads=()):
    if bias is None:
        P.issue("scalar", lambda e: e.activation(out=out_ap, in_=in_ap, func=func, scale=scale),
                reads=(in_,) + tuple(extra_reads), writes=(out,))
    else:
        P.issue("scalar", lambda e: e.activation(out=out_ap, in_=in_ap, func=func, scale=scale, bias=bias),
                reads=(in_,) + tuple(extra_reads), writes=(out,))


def tt(P, out, out_ap, a, a_ap, b, b_ap, op, eng="vector"):
    P.issue(eng, lambda e: e.tensor_tensor(out=out_ap, in0=a_ap, in1=b_ap, op=op),
            reads=(a, b), writes=(out,))


def ts(P, out, out_ap, a, a_ap, s1, op0, s2=None, op1=None, extra_reads=(), eng="vector"):
    if op1 is None:
        P.issue(eng, lambda e: e.tensor_scalar(out=out_ap, in0=a_ap, scalar1=s1, scalar2=None, op0=op0),
                reads=(a,) + tuple(extra_reads), writes=(out,))
    else:
        P.issue(eng, lambda e: e.tensor_scalar(out=out_ap, in0=a_ap, scalar1=s1, scalar2=s2, op0=op0, op1=op1),
                reads=(a,) + tuple(extra_reads), writes=(out,))


def stt(P, out, out_ap, a, a_ap, scalar, b, b_ap, op0, op1, extra_reads=(), eng="vector"):
    P.issue(eng, lambda e: e.scalar_tensor_tensor(out=out_ap, in0=a_ap, scalar=scalar, in1=b_ap, op0=op0, op1=op1),
            reads=(a, b) + tuple(extra_reads), writes=(out,))


def rsqrt(P, out, out_ap, in_, in_ap, scale, eps_ap, eps_obj):
    act(P, out, out_ap, in_, in_ap, AF.Sqrt, scale=scale, bias=eps_ap, extra_reads=(eps_obj,))
    P.issue("vector", lambda e: e.reciprocal(out=out_ap, in_=out_ap), reads=(out,), writes=(out,))


def make_eps(P):
    t = P.sb([128, 4], P.uid("epsc"))
    P.issue("vector", lambda e: e.memset(t[:, 0:1], EPS), writes=(t,))
    P.issue("vector", lambda e: e.memset(t[:, 1:2], 64e-5), writes=(t,))
    P.issue("vector", lambda e: e.memset(t[:, 2:3], 1.0), writes=(t,))
    return t


def rstd_from_ss_old(P, rstd, ss_ps, n, nt, eps=EPS):
    ts(P, rstd, rstd[:, :nt], ss_ps, ss_ps[:, :nt], 1.0 / n, ALU.mult, eps, ALU.add)
    ts(P, rstd, rstd[:, :nt], rstd, rstd[:, :nt], -0.5, ALU.pow)


def build_dense(nsh, nt):
    nc = bass.Bass("TRN2", target_bir_lowering=False)
    KC = D // 128
    NF = DFF // 128
    xT = nc.dram_tensor("xT", [nsh, D, nt], F32, kind="ExternalInput").ap()
    oT = nc.dram_tensor("oT", [nsh, D, nt], F32, kind="ExternalInput").ap()
    wg = nc.dram_tensor("wg", [4 * KC, 128, KC * 128], F32, kind="ExternalInput").ap()
    wb = nc.dram_tensor("wb", [4 * KC, 128, 8 * 128], F32, kind="ExternalInput").ap()
    wo = nc.dram_tensor("wo", [KC, 128, KC * 128], F32, kind="ExternalInput").ap()
    wu = nc.dram_tensor("wu", [2 * NF, 128, KC * 128], F32, kind="ExternalInput").ap()
    wd = nc.dram_tensor("wd", [KC, 128, NF * 128], F32, kind="ExternalInput").ap()
    gains = nc.dram_tensor("gains", [128, 4 * KC], F32, kind="ExternalInput").ap()
    convw = nc.dram_tensor("convw", [128, 2 * NF * 3], F32, kind="ExternalInput").ap()
    yT = nc.dram_tensor("yT", [nsh, D, nt], F32, kind="ExternalOutput").ap()

    with ExitStack() as stack:
        P = Prog(nc, stack)
        P.epsc = make_eps(P)
        X = P.sb([128, KC, nt], "X")
        H = P.sb([128, KC, nt], "H")
        WR = [P.sb([128, KC * 128], f"wr{i}") for i in range(2)]
        G = P.sb([128, 4 * KC], "gains_sb")
        CW = P.sb([128, 2 * NF * 3], "convw_sb")
        ones = P.sb([128, 128], "ones")
        rstd = P.sb([128, nt], "rstd")
        tmpA = [P.sb([128, nt], f"tmpA{i}") for i in range(2)]
        tmpB = [P.sb([128, nt], f"tmpB{i}") for i in range(2)]
        tmpC = [P.sb([128, nt], f"tmpC{i}") for i in range(2)]
        BIG = P.sb([128, NF, nt], "BIG")
        PS = [P.ps(f"ps{i}") for i in range(8)]
        psi = [0]
        wri = [0]

        def next_ps():
            p = PS[psi[0] % 6]
            psi[0] += 1
            return p

        SSP = [PS[6], PS[7]]

        def next_w():
            w = WR[wri[0] % 2]
            wri[0] += 1
            return w

        P.issue("vector", lambda e: e.memset(ones[:], 1.0), writes=(ones,))
        P.dma(G[:], gains, writes=(G,))
        P.dma(CW[:], convw, writes=(CW,))

        def norm_stats(src, ssp):
            for kc in range(KC):
                sq = tmpA[kc % 2]
                act(P, sq, sq[:, :], src, src[:, kc, :], AF.Square)
                mm(P, ssp, ssp[:, :nt], ones, ones[:, :], sq, sq[:, :], kc == 0, kc == KC - 1)

        for s in range(nsh):
            P.dma(X[:], xT[s].rearrange("(kc p) t -> p kc t", p=128), writes=(X,))
            norm_stats(X, SSP[0])
            rstd_from_ss(P, rstd, SSP[0], D, nt)
            for kc in range(KC):
                stt(P, H, H[:, kc, :], X, X[:, kc, :], G[:, kc:kc + 1], rstd, rstd[:, :], ALU.mult, ALU.mult,
                    extra_reads=(G,))
            for n in range(4):
                On_lo = KC + 8 * (n % 2)
                P.dma(BIG[:, On_lo:On_lo + 8, :],
                      oT[s, n * 1024:(n + 1) * 1024, :].rearrange("(kc p) t -> p kc t", p=128), writes=(BIG,))
                for dt in range(KC):
                    w = next_w()
                    P.dma(w[:], wg[n * KC + dt], writes=(w,))
                    gps = next_ps()
                    for kc in range(KC):
                        mm(P, gps, gps[:, :nt], w, w[:, kc * 128:(kc + 1) * 128], H, H[:, kc, :], kc == 0, kc == KC - 1)
                    w2 = next_w()
                    P.dma(w2[:, :8 * 128], wb[n * KC + dt], writes=(w2,))
                    yps = next_ps()
                    for kc in range(8):
                        mm(P, yps, yps[:, :nt], w2, w2[:, kc * 128:(kc + 1) * 128], BIG, BIG[:, On_lo + kc, :],
                           kc == 0, kc == 7)
                    sig = tmpB[dt % 2]
                    act(P, sig, sig[:, :], gps, gps[:, :nt], AF.Sigmoid)
                    if n == 0:
                        tt(P, BIG, BIG[:, dt, :], sig, sig[:, :], yps, yps[:, :nt], ALU.mult)
                    else:
                        tm = tmpC[dt % 2]
                        tt(P, tm, tm[:, :], sig, sig[:, :], yps, yps[:, :nt], ALU.mult)
                        tt(P, BIG, BIG[:, dt, :], BIG, BIG[:, dt, :], tm, tm[:, :], ALU.add)
            ssp = SSP[1]
            for dt in range(KC):
                w = next_w()
                P.dma(w[:], wo[dt], writes=(w,))
                zps = next_ps()
                for kc in range(KC):
                    mm(P, zps, zps[:, :nt], w, w[:, kc * 128:(kc + 1) * 128], BIG, BIG[:, kc, :], kc == 0, kc == KC - 1)
                act(P, H, H[:, dt, :], zps, zps[:, :nt], AF.Copy)
                sq = tmpA[dt % 2]
                act(P, sq, sq[:, :], zps, zps[:, :nt], AF.Square)
                mm(P, ssp, ssp[:, :nt], ones, ones[:, :], sq, sq[:, :], dt == 0, dt == KC - 1)
            rstd_from_ss(P, rstd, ssp, D, nt)
            for kc in range(KC):
                tm = tmpC[kc % 2]
                stt(P, tm, tm[:, :], H, H[:, kc, :], G[:, KC + kc:KC + kc + 1], rstd, rstd[:, :], ALU.mult, ALU.mult,
                    extra_reads=(G,))
                tt(P, X, X[:, kc, :], X, X[:, kc, :], tm, tm[:, :], ALU.add)
            norm_stats(X, SSP[0])
            rstd_from_ss(P, rstd, SSP[0], D, nt)
            for kc in range(KC):
                stt(P, H, H[:, kc, :], X, X[:, kc, :], G[:, 2 * KC + kc:2 * KC + kc + 1], rstd, rstd[:, :],
                    ALU.mult, ALU.mult, extra_reads=(G,))
            for f in range(NF):
                res = []
                for half in range(2):
                    ft = half * NF + f
                    w = next_w()
                    P.dma(w[:], wu[ft], writes=(w,))
                    ups = next_ps()
                    for kc in range(KC):
                        mm(P, ups, ups[:, :nt], w, w[:, kc * 128:(kc + 1) * 128], H, H[:, kc, :], kc == 0, kc == KC - 1)
                    u = tmpA[half] if half == 0 else tmpB[0]
                    act(P, u, u[:, :], ups, ups[:, :nt], AF.Copy)
                    c = tmpC[half]
                    ts(P, c, c[:, 2:nt], u, u[:, 2:nt], CW[:, ft * 3 + 2:ft * 3 + 3], ALU.mult, extra_reads=(CW,))
                    stt(P, c, c[:, 2:nt], u, u[:, 1:nt - 1], CW[:, ft * 3 + 1:ft * 3 + 2], c, c[:, 2:nt],
                        ALU.mult, ALU.add, extra_reads=(CW,))
                    stt(P, c, c[:, 2:nt], u, u[:, 0:nt - 2], CW[:, ft * 3:ft * 3 + 1], c, c[:, 2:nt],
                        ALU.mult, ALU.add, extra_reads=(CW,))
                    res.append(c)
                sl = tmpB[1]
                act(P, sl, sl[:, 2:nt], res[0], res[0][:, 2:nt], AF.Silu)
                tt(P, BIG, BIG[:, f, 2:nt], sl, sl[:, 2:nt], res[1], res[1][:, 2:nt], ALU.mult)
                if s == 0:
                    P.issue("gpsimd", lambda e, f=f: e.memset(BIG[:, f, 0:2], 0.0), writes=(BIG,))
            ssp = SSP[1]
            for dt in range(KC):
                zps = next_ps()
                f0 = 0
                while f0 < NF:
                    nf = min(KC, NF - f0)
                    w = next_w()
                    P.dma(w[:, :nf * 128], wd[dt, :, f0 * 128:(f0 + nf) * 128], writes=(w,))
                    for j in range(nf):
                        mm(P, zps, zps[:, :nt], w, w[:, j * 128:(j + 1) * 128], BIG, BIG[:, f0 + j, :],
                           f0 + j == 0, f0 + j == NF - 1)
                    f0 += nf
                act(P, H, H[:, dt, :], zps, zps[:, :nt], AF.Copy)
                sq = tmpA[dt % 2]
                act(P, sq, sq[:, :], zps, zps[:, :nt], AF.Square)
                mm(P, ssp, ssp[:, :nt], ones, ones[:, :], sq, sq[:, :], dt == 0, dt == KC - 1)
            rstd_from_ss(P, rstd, ssp, D, nt)
            for kc in range(KC):
                tm = tmpC[kc % 2]
                stt(P, tm, tm[:, :], H, H[:, kc, :], G[:, 3 * KC + kc:3 * KC + kc + 1], rstd, rstd[:, :],
                    ALU.mult, ALU.mult, extra_reads=(G,))
                tt(P, X, X[:, kc, :], X, X[:, kc, :], tm, tm[:, :], ALU.add)
            P.dma(yT[s].rearrange("(kc p) t -> p kc t", p=128), X[:], reads=(X,))
        P.emit()
    return nc


def tile_w(w, kc_inner=True):
    K, N = w.shape
    return np.ascontiguousarray(w.reshape(K // 128, 128, N // 128, 128).transpose(2, 1, 0, 3)).reshape(N // 128, 128, K)


def dense_weights(i, w_branch, w_gate, w_out, w_up, w_down, ffn_conv, pre_mix, post_mix, pre_ffn, post_ffn):
    KC = D // 128
    wgt = tile_w(w_gate[i])
    wbt = np.concatenate([tile_w(w_branch[i, n]) for n in range(4)], axis=0)
    wot = tile_w(w_out[i])
    wut = tile_w(w_up[i])
    wdt = tile_w(w_down[i])
    g = np.stack([pre_mix[i], post_mix[i], pre_ffn[i], post_ffn[i]], 0).reshape(4, KC, 128).transpose(2, 0, 1)
    g = np.ascontiguousarray(g).reshape(128, 4 * KC)
    cw = np.ascontiguousarray(ffn_conv[i].reshape(3, 2 * DFF // 128, 128).transpose(2, 1, 0)).reshape(128, -1)
    return dict(wg=wgt, wb=wbt, wo=wot, wu=wut, wd=wdt, gains=g.astype(np.float32), convw=cw.astype(np.float32))


def run_dense(xT_full, oT_full, wts, nsh_total=16):
    B, _, L = xT_full.shape
    ns = L // nsh_total
    nt = ns + 2
    per_core = B * nsh_total // NCORES
    xp = np.concatenate([np.zeros((B, D, 2), np.float32), xT_full], axis=2)
    op = np.concatenate([np.zeros((B, D, 2), np.float32), oT_full], axis=2)
    in_maps = []
    for c in range(NCORES):
        xs, os_ = [], []
        for j in range(per_core):
            g = c * per_core + j
            b, k = divmod(g, nsh_total)
            xs.append(xp[b, :, k * ns:k * ns + nt])
            os_.append(op[b, :, k * ns:k * ns + nt])
        m = dict(wts)
        m["xT"] = np.ascontiguousarray(np.stack(xs, 0))
        m["oT"] = np.ascontiguousarray(np.stack(os_, 0))
        in_maps.append(m)
    nc = build_dense(per_core, nt)
    res = run_bass_kernel_spmd(nc, in_maps, core_ids=list(range(NCORES)))
    out = np.zeros((B, D, L), np.float32)
    for c in range(NCORES):
        y = res.results[c]["yT"]
        for j in range(per_core):
            g = c * per_core + j
            b, k = divmod(g, nsh_total)
            out[b, :, k * ns:(k + 1) * ns] = y[j][:, 2:]
    return out


NT_IN = 34
PADL = 4
RQ, RK, RV, RG = 0, 2, 4, 6
AQ, AK, AV, AG, ALR = 8, 9, 10, 12, 14
CQ, CK, CV, CZ, CAB = 15, 17, 19, 21, 23
DR, DK, DV, DW, DA, DG = 24, 26, 28, 30, 31, 32


def build_mixer(T):
    nc = bass.Bass("TRN2", target_bir_lowering=False)
    KC = D // 128
    TB = 512
    blocks = [(t0, min(TB, T - t0)) for t0 in range(0, T, TB)]
    xT = nc.dram_tensor("xT", [D, T], F32, kind="ExternalInput").ap()
    win = nc.dram_tensor("win", [NT_IN, 128, KC * 128], F32, kind="ExternalInput").ap()
    gpre = nc.dram_tensor("gpre", [128, KC], F32, kind="ExternalInput").ap()
    rope = nc.dram_tensor("rope", [2, 128, T], F32, kind="ExternalInput").ap()
    pv = nc.dram_tensor("pv", [128, 64], F32, kind="ExternalInput").ap()
    gw2 = nc.dram_tensor("gw2", [16, 128], F32, kind="ExternalInput").ap()
    rw2 = nc.dram_tensor("rw2", [64, 256], F32, kind="ExternalInput").ap()
    ra2 = nc.dram_tensor("ra2", [64, 256], F32, kind="ExternalInput").ap()
    rg2 = nc.dram_tensor("rg2", [256, 256], F32, kind="ExternalInput").ap()
    consts = nc.dram_tensor("consts", [3, 128, 128], F32, kind="ExternalInput").ap()
    outT = nc.dram_tensor("outT", [1024, T], F32, kind="ExternalOutput").ap()
    PV_GAM, PV_RETN, PV_GLAB, PV_GLAN, PV_CONV, PV_ALOG, PV_DTB, PV_GDNN = 0, 1, 3, 4, 6, 30, 31, 32
    PV_MU, PV_W0, PV_A0, PV_KK, PV_KA, PV_RK, PV_LNW, PV_LNB = 33, 43, 45, 47, 49, 51, 53, 55

    with ExitStack() as stack:
        P = Prog(nc, stack)
        P.epsc = make_eps(P)
        PR = P.dram([NT_IN * 128, PADL + T], "PR")
        BRr = P.dram([T, 1, 512], "BRr")
        BRa = P.dram([T, 1, 384], "BRa")
        BRc = P.dram([T, 1, 1280], "BRc")
        BRd = P.dram([T, 2, 640], "BRd")
        VTc = P.dram([128, 2, T], "VTc")
        VTd = P.dram([128, 2, T], "VTd")
        OS = [P.dram([128, 2, T], f"OS{i}") for i in range(4)]
        BON = P.dram([256, T], "BON")
        GT = P.dram([256, T], "GT")
        OUT = Obj(outT, "outT")

        PVs = P.sb([128, 64], "pv_sb")
        CN = P.sb([128, 3, 128], "consts_sb")
        Gp = P.sb([128, KC], "gpre_sb")
        PS = [P.ps(f"ps{i}") for i in range(8)]
        psi = [0]

        def next_ps():
            p = PS[psi[0] % 7]
            psi[0] += 1
            return p
        SSP = PS[7]
        P.dma(PVs[:], pv, writes=(PVs,))
        P.dma(CN[:], consts.rearrange("a p c -> p a c"), writes=(CN,))
        P.dma(Gp[:], gpre, writes=(Gp,))
        ident, ones, bones = CN[:, 0, :], CN[:, 1, :], CN[:, 2, :]
        act(P, PVs, PVs[:, 30:31], PVs, PVs[:, 30:31], AF.Exp)
        ts(P, PVs, PVs[:, 30:31], PVs, PVs[:, 30:31], -1.0, ALU.mult)

        def pcol(c):
            return PVs[:, c:c + 1]

        with ExitStack() as st1:
            P.stack = st1
            X = P.sb([128, KC, TB], "X")
            H = P.sb([128, KC, TB], "H")
            WR = [P.sb([128, KC * 128], f"wr{i}") for i in range(2)]
            sqt = [P.sb([128, TB], f"sq{i}") for i in range(2)]
            ev = [P.sb([128, TB], f"ev{i}") for i in range(2)]
            rstd = P.sb([128, TB], "rstd")
            zt = P.sb([128, PADL], "zt")
            P.issue("vector", lambda e: e.memset(zt[:], 0.0), writes=(zt,))
            for ct in range(NT_IN):
                P.dma(PR[ct * 128:(ct + 1) * 128, 0:PADL], zt[:], reads=(zt,), writes=(PR,))
            wi = 0
            for (t0, tb) in blocks:
                P.dma(X[:, :, :tb], xT[:, t0:t0 + tb].rearrange("(kc p) t -> p kc t", p=128), writes=(X,))
                for kc in range(KC):
                    sq = sqt[kc % 2]
                    act(P, sq, sq[:, :tb], X, X[:, kc, :tb], AF.Square)
                    mm(P, SSP, SSP[:, :tb], CN, ones, sq, sq[:, :tb], kc == 0, kc == KC - 1)
                rstd_from_ss(P, rstd, SSP, D, tb)
                for kc in range(KC):
                    stt(P, H, H[:, kc, :tb], X, X[:, kc, :tb], Gp[:, kc:kc + 1], rstd, rstd[:, :tb],
                        ALU.mult, ALU.mult, extra_reads=(Gp,))
                for ct in range(NT_IN):
                    w = WR[wi % 2]
                    wi += 1
                    P.dma(w[:], win[ct], writes=(w,))
                    ps = next_ps()
                    for kc in range(KC):
                        mm(P, ps, ps[:, :tb], w, w[:, kc * 128:(kc + 1) * 128], H, H[:, kc, :tb], kc == 0, kc == KC - 1)
                    e_ = ev[ct % 2]
                    act(P, e_, e_[:, :tb], ps, ps[:, :tb], AF.Copy)
                    P.dma(PR[ct * 128:(ct + 1) * 128, PADL + t0:PADL + t0 + tb], e_[:, :tb], reads=(e_,), writes=(PR,))
        P.stack = stack

        with ExitStack() as st2:
            P.stack = st2
            NTMP = 14
            tp = [P.sb([128, TB + PADL], f"tp{i}") for i in range(NTMP)]
            rows = [P.sb([128, 128], f"rows{i}") for i in range(4)]
            P_rows5 = P.sb([128, 5, 128], "rows5")
            ri = [0]

            def load(dst, tile_idx, t0, tb, halo=0, nrows=128):
                P.dma(dst[:nrows, :tb + halo],
                      PR[tile_idx * 128:tile_idx * 128 + nrows, PADL + t0 - halo:PADL + t0 + tb], reads=(PR,), writes=(dst,))

            def to_rows(src, tb, dst_obj, dst_fn):
                for s0 in range(0, tb, 128):
                    n = min(128, tb - s0)
                    ps = next_ps()
                    P.issue("tensor", lambda e, ps=ps, s0=s0, n=n: e.transpose(ps[:n, :128], src[:, s0:s0 + n], ident),
                            reads=(src, CN), writes=(ps,))
                    r = rows[ri[0] % 4]
                    ri[0] += 1
                    act(P, r, r[:n, :], ps, ps[:n, :128], AF.Copy)
                    yield r, s0, n

            def rows_out(src, tb, dst, t0, ph, off, width=128, c0=0):
                for r, s0, n in to_rows(src, tb, dst, None):
                    P.dma(dst[t0 + s0:t0 + s0 + n, ph, off:off + width], r[:n, c0:c0 + width], reads=(r,), writes=(dst,))

            def group_sum(dst, srcs, tb, which):
                for i, s_ in enumerate(srcs):
                    mm(P, dst, dst[:, :tb], CN, which, s_, s_[:, :tb], i == 0, i == len(srcs) - 1)

            for (t0, tb) in blocks:
                q1, q2, k1, k2, cs, sn, a_, b_, c_, d_ = tp[:10]
                load(q1, RQ, t0, tb); load(q2, RQ + 1, t0, tb); load(k1, RK, t0, tb); load(k2, RK + 1, t0, tb)
                P.dma(cs[:, :tb], rope[0, :, t0:t0 + tb], writes=(cs,))
                P.dma(sn[:, :tb], rope[1, :, t0:t0 + tb], writes=(sn,))
                for (x1, x2, off, scl) in ((k1, k2, 0, 1.0), (q1, q2, 256, 256 ** -0.5)):
                    tt(P, a_, a_[:, :tb], x1, x1[:, :tb], cs, cs[:, :tb], ALU.mult)
                    tt(P, b_, b_[:, :tb], x2, x2[:, :tb], sn, sn[:, :tb], ALU.mult)
                    tt(P, a_, a_[:, :tb], a_, a_[:, :tb], b_, b_[:, :tb], ALU.subtract)
                    tt(P, c_, c_[:, :tb], x1, x1[:, :tb], sn, sn[:, :tb], ALU.mult)
                    tt(P, d_, d_[:, :tb], x2, x2[:, :tb], cs, cs[:, :tb], ALU.mult)
                    tt(P, c_, c_[:, :tb], c_, c_[:, :tb], d_, d_[:, :tb], ALU.add)
                    if scl != 1.0:
                        ts(P, a_, a_[:, :tb], a_, a_[:, :tb], scl, ALU.mult)
                        ts(P, c_, c_[:, :tb], c_, c_[:, :tb], scl, ALU.mult)
                    rows_out(a_, tb, BRr, t0, 0, off)
                    rows_out(c_, tb, BRr, t0, 0, off + 128)
                lr, al, kk_, qq_ = tp[:4]
                load(lr, ALR, t0, tb, nrows=16)
                g2t = tp[4]
                P.dma(g2t[:16, :128], gw2, writes=(g2t,))
                ps = next_ps()
                mm(P, ps, ps[:, :tb], g2t, g2t[:16, :128], lr, lr[:16, :tb], True, True)
                nb_ = tp[5]
                ts(P, nb_, nb_[:, 0:1], PVs, pcol(PV_GLAB), -1.0, ALU.mult)
                act(P, al, al[:, :tb], ps, ps[:, :tb], AF.Exp, scale=-1.0, bias=nb_[:, 0:1], extra_reads=(nb_,))
                act(P, al, al[:, :tb], al, al[:, :tb], AF.Ln, scale=1.0, bias=P.epsc[:, 2:3], extra_reads=(P.epsc,))
                act(P, al, al[:, :tb], al, al[:, :tb], AF.Exp, scale=-1.0 / 16.0)
                rows_out(al, tb, BRa, t0, 0, 0)
                load(kk_, AK, t0, tb)
                rows_out(kk_, tb, BRa, t0, 0, 128)
                load(qq_, AQ, t0, tb)
                ts(P, qq_, qq_[:, :tb], qq_, qq_[:, :tb], 128 ** -0.5, ALU.mult)
                rows_out(qq_, tb, BRa, t0, 0, 256)
                ab, eg = tp[0], tp[1]
                load(ab, CAB, t0, tb)
                act(P, eg, eg[:, :tb], ab, ab[:, :tb], AF.Exp, scale=1.0, bias=pcol(PV_DTB), extra_reads=(PVs,))
                act(P, eg, eg[:, :tb], eg, eg[:, :tb], AF.Ln, scale=1.0, bias=P.epsc[:, 2:3], extra_reads=(P.epsc,))
                ts(P, eg, eg[:, :tb], eg, eg[:, :tb], pcol(PV_ALOG), ALU.mult, extra_reads=(PVs,))
                act(P, eg, eg[:, :tb], eg, eg[:, :tb], AF.Exp)
                bt = tp[2]
                act(P, bt, bt[:, :tb], ab, ab[:, :tb], AF.Sigmoid)
                egr = [None] * 8
                cv = {}
                for nm, base, pvoff in (("q", CQ, 0), ("k", CK, 2), ("v", CV, 4)):
                    for hh in range(2):
                        src = tp[3]
                        load(src, base + hh, t0, tb, halo=3)
                        dst = tp[4 + len(cv)]
                        wc = PV_CONV + (pvoff + hh) * 4
                        ts(P, dst, dst[:, :tb], src, src[:, 3:3 + tb], pcol(wc + 3), ALU.mult, extra_reads=(PVs,))
                        for j in range(3):
                            stt(P, dst, dst[:, :tb], src, src[:, j:j + tb], pcol(wc + j), dst, dst[:, :tb],
                                ALU.mult, ALU.add, extra_reads=(PVs,))
                        act(P, dst, dst[:, :tb], dst, dst[:, :tb], AF.Silu)
                        cv[(nm, hh)] = dst
                sqq = tp[10]
                rn = tp[11]
                for nm, scl in (("q", 128 ** -0.5), ("k", 1.0)):
                    for hh in range(2):
                        x_ = cv[(nm, hh)]
                        tt(P, sqq, sqq[:, :tb], x_, x_[:, :tb], x_, x_[:, :tb], ALU.mult)
                        ps = next_ps()
                        group_sum(ps, [sqq], tb, ones)
                        rsqrt(P, rn, rn[:, :tb], ps, ps[:, :tb], 1.0, P.epsc[:, 0:1], P.epsc)
                        if scl != 1.0:
                            ts(P, rn, rn[:, :tb], rn, rn[:, :tb], scl, ALU.mult)
                        tt(P, x_, x_[:, :tb], x_, x_[:, :tb], rn, rn[:, :tb], ALU.mult)
                for hh in range(2):
                    v_ = cv[("v", hh)]
                    P.dma(VTc[:, hh, t0:t0 + tb], v_[:, :tb], reads=(v_,), writes=(VTc,))
                for s0 in range(0, tb, 128):
                    n = min(128, tb - s0)
                    pe = next_ps()
                    P.issue("tensor", lambda e, pe=pe, s0=s0, n=n: e.transpose(pe[:n, :128], eg[:, s0:s0 + n], ident),
                            reads=(eg, CN), writes=(pe,))
                    pb = next_ps()
                    P.issue("tensor", lambda e, pb=pb, s0=s0, n=n: e.transpose(pb[:n, :128], bt[:, s0:s0 + n], ident),
                            reads=(bt, CN), writes=(pb,))
                    sc = rows[ri[0] % 4]; ri[0] += 1
                    act(P, sc, sc[:n, 0:4], pe, pe[:n, 0:4], AF.Copy)
                    P.issue("scalar", lambda e, sc=sc, pb=pb, n=n: e.activation(out=sc[:n, 2:4], in_=pb[:n, 2:4], func=AF.Copy),
                            reads=(pb,), writes=(sc,))
                    for hh in range(2):
                        pk = next_ps()
                        kx = cv[("k", hh)]
                        P.issue("tensor", lambda e, pk=pk, kx=kx, s0=s0, n=n: e.transpose(pk[:n, :128], kx[:, s0:s0 + n], ident),
                                reads=(kx, CN), writes=(pk,))
                        r5 = P_rows5
                        ts(P, r5, r5[:n, 0, :], CN, ones[:n, :], sc[:n, hh:hh + 1], ALU.mult, extra_reads=(sc,))
                        act(P, r5, r5[:n, 1, :], pk, pk[:n, :128], AF.Copy)
                        ts(P, r5, r5[:n, 3, :], pk, pk[:n, :128], sc[:n, 2 + hh:3 + hh], ALU.mult, extra_reads=(sc,))
                        ts(P, r5, r5[:n, 2, :], r5, r5[:n, 3, :], sc[:n, hh:hh + 1], ALU.mult, extra_reads=(sc,))
                        ts(P, r5, r5[:n, 2, :], r5, r5[:n, 2, :], -1.0, ALU.mult)
                        pq = next_ps()
                        qx = cv[("q", hh)]
                        P.issue("tensor", lambda e, pq=pq, qx=qx, s0=s0, n=n: e.transpose(pq[:n, :128], qx[:, s0:s0 + n], ident),
                                reads=(qx, CN), writes=(pq,))
                        act(P, r5, r5[:n, 4, :], pq, pq[:n, :128], AF.Copy)
                        P.dma(BRc[t0 + s0:t0 + s0 + n, 0, :].rearrange("t (v g k) -> t v g k", v=5, g=2)[:, :, hh, :],
                              r5[:n, :, :], reads=(r5,), writes=(BRc,))
                sh = {}
                for nm, base, ntile, mu0 in (("r", DR, 2, 0), ("k", DK, 2, 2), ("v", DV, 2, 4), ("w", DW, 1, 6),
                                             ("a", DA, 1, 7), ("g", DG, 2, 8)):
                    for i in range(ntile):
                        src = tp[13]
                        load(src, base + i, t0, tb, halo=1)
                        dst = tp[len(sh)]
                        tt(P, dst, dst[:, :tb], src, src[:, 0:tb], src, src[:, 1:1 + tb], ALU.subtract)
                        stt(P, dst, dst[:, :tb], dst, dst[:, :tb], pcol(PV_MU + mu0 + i), src, src[:, 1:1 + tb],
                            ALU.mult, ALU.add, extra_reads=(PVs,))
                        sh[(nm, i)] = dst
                wt_ = tp[10]
                lowr = tp[11]
                act(P, sh[("w", 0)], sh[("w", 0)][:, :tb], sh[("w", 0)], sh[("w", 0)][:, :tb], AF.Tanh)
                for i in range(2):
                    act(P, sh[("g", i)], sh[("g", i)][:, :tb], sh[("g", i)], sh[("g", i)][:, :tb], AF.Sigmoid)
                for i in range(2):
                    r_, k_, v_ = sh[("r", i)], sh[("k", i)], sh[("v", i)]
                    P.dma(VTd[:, i, t0:t0 + tb], v_[:, :tb], reads=(v_,), writes=(VTd,))
                    P.dma(lowr[:64, :128], rw2[:, i * 128:(i + 1) * 128], writes=(lowr,))
                    ps = next_ps()
                    mm(P, ps, ps[:, :tb], lowr, lowr[:64, :128], sh[("w", 0)], sh[("w", 0)][:64, :tb], True, True)
                    dec = tp[12]
                    act(P, dec, dec[:, :tb], ps, ps[:, :tb], AF.Sigmoid, bias=pcol(PV_W0 + i), extra_reads=(PVs,))
                    act(P, dec, dec[:, :tb], dec, dec[:, :tb], AF.Exp, scale=-0.606531)
                    for hh in range(2):
                        rows_out(dec, tb, BRd, t0, hh, (0 * 2 + i) * 64, width=64, c0=hh * 64)
                    P.dma(lowr[:64, :128], ra2[:, i * 128:(i + 1) * 128], writes=(lowr,))
                    ps = next_ps()
                    mm(P, ps, ps[:, :tb], lowr, lowr[:64, :128], sh[("a", 0)], sh[("a", 0)][:64, :tb], True, True)
                    aa = tp[12]
                    act(P, aa, aa[:, :tb], ps, ps[:, :tb], AF.Sigmoid, bias=pcol(PV_A0 + i), extra_reads=(PVs,))
                    ts(P, wt_, wt_[:, :tb], k_, k_[:, :tb], pcol(PV_KK + i), ALU.mult, extra_reads=(PVs,))
                    sq_ = tp[13]
                    tt(P, sq_, sq_[:, :tb], wt_, wt_[:, :tb], wt_, wt_[:, :tb], ALU.mult)
                    ps = next_ps()
                    group_sum(ps, [sq_], tb, bones)
                    rsqrt(P, sq_, sq_[:, :tb], ps, ps[:, :tb], 1.0, P.epsc[:, 0:1], P.epsc)
                    tt(P, wt_, wt_[:, :tb], wt_, wt_[:, :tb], sq_, sq_[:, :tb], ALU.mult)
                    for hh in range(2):
                        rows_out(wt_, tb, BRd, t0, hh, (1 * 2 + i) * 64, width=64, c0=hh * 64)
                    tt(P, wt_, wt_[:, :tb], wt_, wt_[:, :tb], aa, aa[:, :tb], ALU.mult)
                    ts(P, wt_, wt_[:, :tb], wt_, wt_[:, :tb], -1.0, ALU.mult)
                    for hh in range(2):
                        rows_out(wt_, tb, BRd, t0, hh, (2 * 2 + i) * 64, width=64, c0=hh * 64)
                    ts(P, aa, aa[:, :tb], aa, aa[:, :tb], -1.0, ALU.add)
                    ts(P, aa, aa[:, :tb], aa, aa[:, :tb], pcol(PV_KA + i), ALU.mult, extra_reads=(PVs,))
                    ts(P, aa, aa[:, :tb], aa, aa[:, :tb], 1.0, ALU.add)
                    tt(P, k_, k_[:, :tb], k_, k_[:, :tb], aa, aa[:, :tb], ALU.mult)
                    for hh in range(2):
                        rows_out(k_, tb, BRd, t0, hh, (3 * 2 + i) * 64, width=64, c0=hh * 64)
                        rows_out(r_, tb, BRd, t0, hh, (4 * 2 + i) * 64, width=64, c0=hh * 64)
                    tt(P, sq_, sq_[:, :tb], r_, r_[:, :tb], k_, k_[:, :tb], ALU.mult)
                    ts(P, sq_, sq_[:, :tb], sq_, sq_[:, :tb], pcol(PV_RK + i), ALU.mult, extra_reads=(PVs,))
                    ps = next_ps()
                    group_sum(ps, [sq_], tb, bones)
                    tt(P, sq_, sq_[:, :tb], ps, ps[:, :tb], v_, v_[:, :tb], ALU.mult)
                    P.dma(BON[i * 128:(i + 1) * 128, t0:t0 + tb], sq_[:, :tb], reads=(sq_,), writes=(BON,))
                    ps = next_ps()
                    P.dma(lowr[:, :128], rg2[0:128, i * 128:(i + 1) * 128], writes=(lowr,))
                    mm(P, ps, ps[:, :tb], lowr, lowr[:, :128], sh[("g", 0)], sh[("g", 0)][:, :tb], True, False)
                    lowr2 = tp[12]
                    P.dma(lowr2[:, :128], rg2[128:256, i * 128:(i + 1) * 128], writes=(lowr2,))
                    mm(P, ps, ps[:, :tb], lowr2, lowr2[:, :128], sh[("g", 1)], sh[("g", 1)][:, :tb], False, True)
                    act(P, sq_, sq_[:, :tb], ps, ps[:, :tb], AF.Copy)
                    P.dma(GT[i * 128:(i + 1) * 128, t0:t0 + tb], sq_[:, :tb], reads=(sq_,), writes=(GT,))
        P.stack = stack

        def scan(eng, BR, PH, Wd, vsrc_fn, OD, G, K, vecs, lowrank, shared, gam=None):
            TC = 8
            with ExitStack() as st3:
                P.stack = st3
                S = P.sb([128, G, K], P.uid("S"))
                tmp = P.sb([128, G, K], P.uid("stmp"))
                sa = P.sb([128, G], P.uid("sa"))
                RB = [P.sb([128, TC, Wd], P.uid("rb")) for _ in range(2)]
                VB = [P.sb([128, G, TC], P.uid("vb")) for _ in range(2)]
                OB = [P.sb([128, G, TC], P.uid("ob")) for _ in range(2)]
                P.issue(eng, lambda e: e.memset(S[:], 0.0), writes=(S,))
                for ci, c0 in enumerate(range(0, T, TC)):
                    n = min(TC, T - c0)
                    rb, vb, ob = RB[ci % 2], VB[ci % 2], OB[ci % 2]
                    for ph in range(PH):
                        np_ = 128 // PH
                        P.dma(rb[ph * np_:(ph + 1) * np_, :n, :],
                              BR[c0:c0 + n, ph, :].partition_broadcast(np_), reads=(BR,), writes=(rb,))
                    vsrc_fn(vb, c0, n)
                    for i in range(n):
                        def row(nm):
                            v = vecs[nm]
                            if shared:
                                return rb[:, i, v * K:(v + 1) * K].unsqueeze(1).broadcast_to([128, G, K])
                            return rb[:, i, v * G * K:(v + 1) * G * K].rearrange("p (g k) -> p g k", g=G)
                        if lowrank:
                            tt(P, tmp, tmp[:], S, S[:], rb, row("kk"), ALU.mult, eng=eng)
                            P.issue(eng, lambda e: e.tensor_reduce(out=sa[:], in_=tmp[:], axis=AX.X, op=ALU.add),
                                    reads=(tmp,), writes=(sa,))
                        if gam is not None:
                            ts(P, S, S[:], S, S[:], gam, ALU.mult, extra_reads=(PVs,), eng=eng)
                        else:
                            tt(P, S, S[:], S, S[:], rb, row("w"), ALU.mult, eng=eng)
                        if lowrank:
                            tt(P, tmp, tmp[:], rb, row("nb"), sa, sa[:].unsqueeze(2).broadcast_to([128, G, K]), ALU.mult, eng=eng)
                            tt(P, S, S[:], S, S[:], tmp, tmp[:], ALU.add, eng=eng)
                        tt(P, tmp, tmp[:], rb, row("kp"), vb, vb[:, :, i:i + 1].broadcast_to([128, G, K]), ALU.mult, eng=eng)
                        tt(P, S, S[:], S, S[:], tmp, tmp[:], ALU.add, eng=eng)
                        tt(P, tmp, tmp[:], S, S[:], rb, row("q"), ALU.mult, eng=eng)
                        P.issue(eng, lambda e, ob=ob, i=i: e.tensor_reduce(out=ob[:, :, i], in_=tmp[:], axis=AX.X, op=ALU.add),
                                reads=(tmp,), writes=(ob,))
                    P.dma(OD[:, :, c0:c0 + n], ob[:, :, :n], reads=(ob,), writes=(OD,))
            P.stack = stack

        def v_from_pr(tile0):
            def f(vb, c0, n):
                for g in range(2):
                    P.dma(vb[:, g, :n], PR[(tile0 + g) * 128:(tile0 + g + 1) * 128, PADL + c0:PADL + c0 + n],
                          reads=(PR,), writes=(vb,))
            return f

        def v_from_vtc(vb, c0, n):
            P.dma(vb[:, :, :n], VTc[:, :, c0:c0 + n], reads=(VTc,), writes=(vb,))

        def v_from_vtd(vb, c0, n):
            P.dma(vb[:, :, :n], VTd[:, :, c0:c0 + n], reads=(VTd,), writes=(vb,))

        scan("vector", BRr, 1, 512, v_from_pr(RV), OS[0], 2, 256, {"kp": 0, "q": 1}, False, True, gam=pcol(PV_GAM))
        scan("vector", BRc, 1, 1280, v_from_vtc, OS[2], 2, 128, {"w": 0, "kk": 1, "nb": 2, "kp": 3, "q": 4}, True, False)
        scan("vector", BRa, 1, 384, v_from_pr(AV), OS[1], 2, 128, {"w": 0, "kp": 1, "q": 2}, False, True)
        scan("vector", BRd, 2, 640, v_from_vtd, OS[3], 2, 64, {"w": 0, "kk": 1, "nb": 2, "kp": 3, "q": 4}, True, False)

        with ExitStack() as st4:
            P.stack = st4
            tq = [P.sb([128, TB], f"tq{i}") for i in range(10)]
            for (t0, tb) in blocks:
                for m in range(4):
                    o0, o1, g0, g1, sq0, sq1, mean, rn = tq[:8]
                    P.dma(o0[:, :tb], OS[m][:, 0, t0:t0 + tb], reads=(OS[m],), writes=(o0,))
                    P.dma(o1[:, :tb], OS[m][:, 1, t0:t0 + tb], reads=(OS[m],), writes=(o1,))
                    oo = [o0, o1]
                    if m in (0, 1):
                        groups, which, n_el = [[0, 1]], ones, 256.0
                    elif m == 2:
                        groups, which, n_el = [[0], [1]], ones, 128.0
                    else:
                        groups, which, n_el = [[0], [1]], bones, 64.0
                    center = m in (0, 3)
                    eps = 64e-5 if m == 3 else EPS
                    for grp in groups:
                        if center:
                            ps = next_ps()
                            group_sum(ps, [oo[g] for g in grp], tb, which)
                            ts(P, mean, mean[:, :tb], ps, ps[:, :tb], 1.0 / n_el, ALU.mult)
                            for g in grp:
                                tt(P, oo[g], oo[g][:, :tb], oo[g], oo[g][:, :tb], mean, mean[:, :tb], ALU.subtract)
                        sqs = [sq0, sq1]
                        for g in grp:
                            tt(P, sqs[g], sqs[g][:, :tb], oo[g], oo[g][:, :tb], oo[g], oo[g][:, :tb], ALU.mult)
                        ps = next_ps()
                        group_sum(ps, [sqs[g] for g in grp], tb, which)
                        rsqrt(P, rn, rn[:, :tb], ps, ps[:, :tb], 1.0 / n_el, P.epsc[:, 1:2] if m == 3 else P.epsc[:, 0:1], P.epsc)
                        for g in grp:
                            ncol = {0: PV_RETN + g, 1: PV_GLAN + g, 2: PV_GDNN, 3: PV_LNW + g}[m]
                            stt(P, oo[g], oo[g][:, :tb], oo[g], oo[g][:, :tb], pcol(ncol), rn, rn[:, :tb],
                                ALU.mult, ALU.mult, extra_reads=(PVs,))
                    for g in range(2):
                        gt_ = [g0, g1][g]
                        if m == 3:
                            ts(P, oo[g], oo[g][:, :tb], oo[g], oo[g][:, :tb], pcol(PV_LNB + g), ALU.add, extra_reads=(PVs,))
                            P.dma(gt_[:, :tb], BON[g * 128:(g + 1) * 128, t0:t0 + tb], reads=(BON,), writes=(gt_,))
                            tt(P, oo[g], oo[g][:, :tb], oo[g], oo[g][:, :tb], gt_, gt_[:, :tb], ALU.add)
                            gt2 = tq[8]
                            P.dma(gt2[:, :tb], GT[g * 128:(g + 1) * 128, t0:t0 + tb], reads=(GT,), writes=(gt2,))
                            tt(P, oo[g], oo[g][:, :tb], oo[g], oo[g][:, :tb], gt2, gt2[:, :tb], ALU.mult)
                        else:
                            gtile = {0: RG, 1: AG, 2: CZ}[m] + g
                            P.dma(gt_[:, :tb], PR[gtile * 128:(gtile + 1) * 128, PADL + t0:PADL + t0 + tb],
                                  reads=(PR,), writes=(gt_,))
                            act(P, gt_, gt_[:, :tb], gt_, gt_[:, :tb], AF.Silu)
                            tt(P, oo[g], oo[g][:, :tb], oo[g], oo[g][:, :tb], gt_, gt_[:, :tb], ALU.mult)
                        P.dma(outT[m * 256 + g * 128:m * 256 + (g + 1) * 128, t0:t0 + tb], oo[g][:, :tb],
                              reads=(oo[g],), writes=(OUT,))
        P.stack = stack
        P.emit()
    return nc


def mixer_cols(jq):
    A_q, A_k, A_v, A_g = 0, 1024, 2048, 3072
    B_q, B_k, B_v, B_g, B_lr = 4096, 4608, 5120, 6144, 7168
    C_q, C_k, C_v, C_z, C_a, C_b = 7184, 8208, 9232, 10256, 11280, 11288
    Ds = 11296
    groups = [
        (np.arange(A_q + jq * 256, A_q + jq * 256 + 256), 256), (np.arange(A_k + jq * 256, A_k + jq * 256 + 256), 256),
        (np.arange(A_v + jq * 256, A_v + jq * 256 + 256), 256), (np.arange(A_g + jq * 256, A_g + jq * 256 + 256), 256),
        (np.arange(B_q + jq * 128, B_q + jq * 128 + 128), 128), (np.arange(B_k + jq * 128, B_k + jq * 128 + 128), 128),
        (np.arange(B_v + jq * 256, B_v + jq * 256 + 256), 256), (np.arange(B_g + jq * 256, B_g + jq * 256 + 256), 256),
        (np.arange(B_lr, B_lr + 16), 128),
        (np.arange(C_q + jq * 256, C_q + jq * 256 + 256), 256), (np.arange(C_k + jq * 256, C_k + jq * 256 + 256), 256),
        (np.arange(C_v + jq * 256, C_v + jq * 256 + 256), 256), (np.arange(C_z + jq * 256, C_z + jq * 256 + 256), 256),
        (np.array([C_a + 2 * jq, C_a + 2 * jq + 1, C_b + 2 * jq, C_b + 2 * jq + 1]), 128),
        (np.arange(Ds + jq * 256, Ds + jq * 256 + 256), 256), (np.arange(Ds + 1024 + jq * 256, Ds + 1024 + jq * 256 + 256), 256),
        (np.arange(Ds + 2048 + jq * 256, Ds + 2048 + jq * 256 + 256), 256),
        (np.arange(Ds + 3072, Ds + 3136), 128), (np.arange(Ds + 3136, Ds + 3200), 128), (np.arange(Ds + 3200, Ds + 3360), 256),
    ]
    return groups


def mixer_params(i, jq, T, p):
    groups = mixer_cols(jq)
    w_in = p["w_in"][i]
    W = np.zeros((D, NT_IN * 128), np.float32)
    off = 0
    for cols, width in groups:
        W[:, off:off + len(cols)] = w_in[:, cols]
        off += width
    assert off == NT_IN * 128
    pv = np.zeros((128, 64), np.float32)
    pp = np.arange(128)
    pv[:, 0] = 1.0 - 2.0 ** (-5.0 - jq)
    for g in range(2):
        pv[:, 1 + g] = p["ret_norm"][i][jq * 256 + g * 128 + pp]
        pv[:, 4 + g] = p["gla_norm"][i][jq * 256 + g * 128 + pp]
    pv[:, 3] = p["gla_b"][i][jq * 128 + pp]
    for pvoff, base in ((0, 0), (2, 1024), (4, 2048)):
        for hh in range(2):
            for j in range(4):
                pv[:, 6 + (pvoff + hh) * 4 + j] = p["gdn_conv"][i][j, base + (2 * jq + hh) * 128 + pp]
    pv[0:2, 30] = p["gdn_a_log"][i][2 * jq:2 * jq + 2]
    pv[0:2, 31] = p["gdn_dt_bias"][i][2 * jq:2 * jq + 2]
    pv[:, 32] = p["gdn_norm"][i]
    mu = p["rwkv_mu"][i]
    for t_ in range(2):
        pv[:, 33 + t_] = mu[0 + jq * 256 + t_ * 128 + pp]
        pv[:, 35 + t_] = mu[1024 + jq * 256 + t_ * 128 + pp]
        pv[:, 37 + t_] = mu[2048 + jq * 256 + t_ * 128 + pp]
    pv[:64, 39] = mu[3072:3136]
    pv[:64, 40] = mu[3136:3200]
    pv[:, 41] = mu[3200:3328]
    pv[:32, 42] = mu[3328:3360]
    for t_ in range(2):
        sl = jq * 256 + t_ * 128 + pp
        pv[:, 43 + t_] = p["rwkv_w0"][i][sl]
        pv[:, 45 + t_] = p["rwkv_a0"][i][sl]
        pv[:, 47 + t_] = p["rwkv_kk"][i][sl]
        pv[:, 49 + t_] = p["rwkv_ka"][i][sl]
        pv[:, 51 + t_] = p["rwkv_rk"][i].reshape(-1)[sl]
        pv[:, 53 + t_] = p["rwkv_ln_w"][i][sl]
        pv[:, 55 + t_] = p["rwkv_ln_b"][i][sl]
    rg2 = np.zeros((256, 256), np.float32)
    rg2[:160] = p["rwkv_g2"][i][:, jq * 256:(jq + 1) * 256]
    gpre = np.ascontiguousarray(p["pre_mix"][i].reshape(D // 128, 128).T)
    return dict(win=tile_w(W), pv=pv, gpre=gpre.astype(np.float32),
                gw2=np.ascontiguousarray(p["gla_w2"][i][:, jq * 128:(jq + 1) * 128]),
                rw2=np.ascontiguousarray(p["rwkv_w2"][i][:, jq * 256:(jq + 1) * 256]),
                ra2=np.ascontiguousarray(p["rwkv_a2"][i][:, jq * 256:(jq + 1) * 256]), rg2=rg2)


def const_tables(T):
    half = 128
    inv_freq = (10000.0 ** (-np.arange(half, dtype=np.float32) / half)).astype(np.float32)
    ang = (np.arange(T, dtype=np.float32)[None, :] * inv_freq[:, None]).astype(np.float32)
    rope = np.stack([np.cos(ang), np.sin(ang)], 0).astype(np.float32)
    bones = np.zeros((128, 128), np.float32)
    bones[:64, :64] = 1.0
    bones[64:, 64:] = 1.0
    consts = np.stack([np.eye(128, dtype=np.float32), np.ones((128, 128), np.float32), bones], 0)
    return rope, consts


def run_mixer(i, hT, p):
    B, _, T = hT.shape
    rope, consts = const_tables(T)
    in_maps = []
    for c in range(NCORES):
        b, jq = divmod(c, 4)
        m = mixer_params(i, jq, T, p)
        m["xT"] = np.ascontiguousarray(hT[b])
        m["rope"] = rope
        m["consts"] = consts
        in_maps.append(m)
    nc = build_mixer(T)
    res = run_bass_kernel_spmd(nc, in_maps, core_ids=list(range(NCORES)))
    oT = np.zeros((B, D, T), np.float32)
    for c in range(NCORES):
        b, jq = divmod(c, 4)
        o = res.results[c]["outT"]
        for m_ in range(4):
            oT[b, m_ * 1024 + jq * 256:m_ * 1024 + (jq + 1) * 256] = o[m_ * 256:(m_ + 1) * 256]
    return oT


def kernel(**inp):
    inp = {k: np.asarray(v) for k, v in inp.items()}
    x = inp["x"].astype(np.float32)
    B, S, _ = x.shape
    meta = np.broadcast_to(inp["meta"][None], (B, NMETA, D))
    h = np.concatenate([meta, x], axis=1)
    hT = np.ascontiguousarray(h.transpose(0, 2, 1))
    for i in range(2):
        oT = run_mixer(i, hT, inp)
        wts = dense_weights(i, inp["w_branch"], inp["w_gate"], inp["w_out"], inp["w_up"], inp["w_down"],
                            inp["ffn_conv"], inp["pre_mix"], inp["post_mix"], inp["pre_ffn"], inp["post_ffn"])
        hT = run_dense(hT, oT, wts)
    return np.ascontiguousarray(hT[:, :, NMETA:].transpose(0, 2, 1)).astype(np.float32)


def rstd_from_ss(P, rstd, ss_ps, n, nt, eps=EPS):
    if not hasattr(P, "epsc"):
        P.epsc = make_eps(P)
    rsqrt(P, rstd, rstd[:, :nt], ss_ps, ss_ps[:, :nt], 1.0 / n, P.epsc[:, 0:1], P.epsc)
```

```python
from contextlib import ExitStack
import numpy as np
import concourse.bass as bass
import concourse.mybir as mybir
from concourse.bass_utils import run_bass_kernel_spmd

F32 = mybir.dt.float32
FR = mybir.dt.float32r
ALU = mybir.AluOpType
AF = mybir.ActivationFunctionType
AX = mybir.AxisListType

D = 4096
DFF = 11008
NMETA = 16
EPS = 1e-6
NCORES = 8
TRACE = False
SKIP_SCANS = False
SKIP_PREP = False
LAST_NS = []


class Obj:
    def __init__(self, t, name):
        self.t = t
        self.name = name
        self.w = {}
        self.r = {}

    def __getitem__(self, k):
        return self.t[k]


class Prog:
    COMPUTE = ("scalar", "vector", "gpsimd", "tensor")
    NDS = 12

    def __init__(self, nc, stack):
        self.nc = nc
        self.stack = stack
        self.ops = {e: [] for e in self.COMPUTE + ("sync",)}
        self.semh = {}
        self.cnt = {}
        for e in self.COMPUTE:
            self.semh["s_" + e] = stack.enter_context(nc.semaphore("s_" + e))
            self.cnt[e] = 0
        self.dcnt = [0] * self.NDS
        self.dnext = 0
        for j in range(self.NDS):
            self.semh[f"d{j}"] = stack.enter_context(nc.semaphore(f"d{j}"))
        self.seen = {e: {} for e in self.ops}
        self.nuid = 0
        self.pending_waits = {}
        self.bar_tiles = {e: self.sb([128, 1], "bar_" + e) for e in ("scalar", "vector", "gpsimd")}

    def uid(self, p):
        self.nuid += 1
        return f"{p}{self.nuid}"

    def sb(self, shape, name=None, dtype=F32):
        name = name or self.uid("t")
        t = self.stack.enter_context(self.nc.sbuf_tensor(name, list(shape), dtype))
        return Obj(t, name)

    def ps(self, name=None):
        name = name or self.uid("p")
        t = self.stack.enter_context(self.nc.psum_tensor(name, [128, 512], F32))
        return Obj(t, name)

    def dram(self, shape, name=None):
        name = name or self.uid("dr")
        t = self.nc.dram_tensor(name, list(shape), F32, kind="Internal")
        return Obj(t.ap(), name)

    def _deps(self, reads, writes):
        waits = {}
        for t in reads:
            for k, v in t.w.items():
                waits[k] = max(waits.get(k, 0), v)
        for t in writes:
            for k, v in list(t.w.items()) + list(t.r.items()):
                waits[k] = max(waits.get(k, 0), v)
        return waits

    def _commit(self, eng, waits, fn, key, inc, val, reads, writes):
        seen = self.seen[eng]
        wl = []
        for k, v in waits.items():
            if seen.get(k, 0) < v:
                seen[k] = v
                wl.append((k, v))
        if self.pending_waits.get(eng):
            wl = self.pending_waits.pop(eng) + wl
        self.ops[eng].append((wl, fn, key, inc))
        for t in writes:
            t.w = {key: val}
            t.r = {}
        for t in reads:
            if t not in writes:
                t.r[key] = max(t.r.get(key, 0), val)

    def issue(self, eng, fn, reads=(), writes=()):
        waits = self._deps(reads, writes)
        key = "s_" + eng
        if eng == "tensor":
            waits.pop(key, None)
        self.cnt[eng] += 1
        self._commit(eng, waits, fn, key, 1, self.cnt[eng], reads, writes)

    def dma(self, out_ap, in_ap, reads=(), writes=(), queue="sync"):
        j = self.dnext
        self.dnext = (j + 1) % self.NDS
        key = f"d{j}"
        waits = self._deps(reads, writes)
        if self.dcnt[j] > 0:
            waits[key] = max(waits.get(key, 0), self.dcnt[j])
        self.dcnt[j] += 16
        self._commit(queue, waits, lambda e, o=out_ap, i=in_ap: e.dma_start(out=o, in_=i),
                     key, 16, self.dcnt[j], reads, writes)

    def barrier(self):
        bars = {}
        for e in self.COMPUTE:
            if not hasattr(self, "bar_tiles"):
                self.bar_tiles = {}
            if e not in self.bar_tiles and e != "tensor":
                self.bar_tiles[e] = self.sb([128, 1], "bar_" + e)
        allw = {"s_" + e: self.cnt[e] for e in self.COMPUTE if self.cnt[e] > 0}
        for j in range(self.NDS):
            if self.dcnt[j] > 0:
                allw[f"d{j}"] = self.dcnt[j]
        for e in ("scalar", "vector", "gpsimd"):
            t = self.bar_tiles[e]
            if e == "scalar":
                fn = lambda en, t=t: en.activation(out=t[:, :], in_=t[:, :], func=AF.Copy)
            else:
                fn = lambda en, t=t: en.memset(t[:, :], 0.0)
            self.cnt[e] += 1
            self._commit(e, dict(allw), fn, "s_" + e, 1, self.cnt[e], (), (t,))
        for e in ("tensor", "sync"):
            seen = self.seen[e]
            pend = [(k, v) for k, v in allw.items() if seen.get(k, 0) < v and k != "s_" + e]
            for k, v in pend:
                seen[k] = v
            self.pending_waits.setdefault(e, []).extend(pend)

    def emit(self):
        nc = self.nc
        finals = [(f"d{j}", self.dcnt[j]) for j in range(self.NDS) if self.dcnt[j] > 0]

        def run(e, name):
            for wl, fn, key, inc in self.ops[name]:
                for k, v in wl:
                    e.wait_ge(self.semh[k], v)
                fn(e).then_inc(self.semh[key], inc)
            if name == "sync":
                for k, v in finals:
                    e.wait_ge(self.semh[k], v)

        with nc.Block() as block:
            @block.sync
            def _(e):
                run(e, "sync")

            @block.scalar
            def _(e):
                run(e, "scalar")

            @block.vector
            def _(e):
                run(e, "vector")

            @block.gpsimd
            def _(e):
                run(e, "gpsimd")

            @block.tensor
            def _(e):
                run(e, "tensor")


def mm(P, out, out_ap, lhsT, lhsT_ap, rhs, rhs_ap, start, stop):
    P.issue("tensor", lambda e: e.matmul(out_ap, lhsT_ap, rhs_ap, start=start, stop=stop),
            reads=(lhsT, rhs), writes=(out,))


def act(P, out, out_ap, in_, in_ap, func, scale=1.0, bias=None, extra_reads=()):
    if bias is None:
        P.issue("scalar", lambda e: e.activation(out=out_ap, in_=in_ap, func=func, scale=scale),
                reads=(in_,) + tuple(extra_reads), writes=(out,))
    else:
        P.issue("scalar", lambda e: e.activation(out=out_ap, in_=in_ap, func=func, scale=scale, bias=bias),
                reads=(in_,) + tuple(extra_reads), writes=(out,))


def tt(P, out, out_ap, a, a_ap, b, b_ap, op, eng="vector"):
    P.issue(eng, lambda e: e.tensor_tensor(out=out_ap, in0=a_ap, in1=b_ap, op=op),
            reads=(a, b), writes=(out,))


def ts(P, out, out_ap, a, a_ap, s1, op0, s2=None, op1=None, extra_reads=(), eng="vector"):
    if op1 is None:
        P.issue(eng, lambda e: e.tensor_scalar(out=out_ap, in0=a_ap, scalar1=s1, scalar2=None, op0=op0),
                reads=(a,) + tuple(extra_reads), writes=(out,))
    else:
        P.issue(eng, lambda e: e.tensor_scalar(out=out_ap, in0=a_ap, scalar1=s1, scalar2=s2, op0=op0, op1=op1),
                reads=(a,) + tuple(extra_reads), writes=(out,))


def stt(P, out, out_ap, a, a_ap, scalar, b, b_ap, op0, op1, extra_reads=(), eng="vector"):
    P.issue(eng, lambda e: e.scalar_tensor_tensor(out=out_ap, in0=a_ap, scalar=scalar, in1=b_ap, op0=op0, op1=op1),
            reads=(a, b) + tuple(extra_reads), writes=(out,))


def rsqrt(P, out, out_ap, in_, in_ap, scale, eps_ap, eps_obj):
    act(P, out, out_ap, in_, in_ap, AF.Sqrt, scale=scale, bias=eps_ap, extra_reads=(eps_obj,))
    P.issue("vector", lambda e: e.reciprocal(out=out_ap, in_=out_ap), reads=(out,), writes=(out,))


def make_eps(P):
    t = P.sb([128, 4], P.uid("epsc"))
    P.issue("vector", lambda e: e.memset(t[:, 0:1], EPS), writes=(t,))
    P.issue("vector", lambda e: e.memset(t[:, 1:2], 64e-5), writes=(t,))
    P.issue("vector", lambda e: e.memset(t[:, 2:3], 1.0), writes=(t,))
    return t


def rstd_from_ss_old(P, rstd, ss_ps, n, nt, eps=EPS):
    ts(P, rstd, rstd[:, :nt], ss_ps, ss_ps[:, :nt], 1.0 / n, ALU.mult, eps, ALU.add)
    ts(P, rstd, rstd[:, :nt], rstd, rstd[:, :nt], -0.5, ALU.pow)


def build_dense(nsh, nt):
    nc = bass.Bass("TRN2", target_bir_lowering=False)
    KC = D // 128
    NF = DFF // 128
    xT = nc.dram_tensor("xT", [nsh, D, nt], F32, kind="ExternalInput").ap()
    oT = nc.dram_tensor("oT", [nsh, D, nt], F32, kind="ExternalInput").ap()
    wg = nc.dram_tensor("wg", [4 * KC, 128, KC * 128], F32, kind="ExternalInput").ap()
    wb = nc.dram_tensor("wb", [4 * KC, 128, 8 * 128], F32, kind="ExternalInput").ap()
    wo = nc.dram_tensor("wo", [KC, 128, KC * 128], F32, kind="ExternalInput").ap()
    wu = nc.dram_tensor("wu", [2 * NF, 128, KC * 128], F32, kind="ExternalInput").ap()
    wd = nc.dram_tensor("wd", [KC, 128, NF * 128], F32, kind="ExternalInput").ap()
    gains = nc.dram_tensor("gains", [128, 4 * KC], F32, kind="ExternalInput").ap()
    convw = nc.dram_tensor("convw", [128, 2 * NF * 3], F32, kind="ExternalInput").ap()
    yT = nc.dram_tensor("yT", [nsh, D, nt], F32, kind="ExternalOutput").ap()

    with ExitStack() as stack:
        P = Prog(nc, stack)
        P.epsc = make_eps(P)
        X = P.sb([128, KC, nt], "X")
        H = P.sb([128, KC, nt], "H", dtype=FR)
        PK = 16
        WS = [P.sb([128, PK * 128], f"ws{i}") for i in range(2)]
        WF = [P.sb([128, PK * 128], f"wf{i}", dtype=FR) for i in range(2)]
        OST = P.sb([128, 8, nt], "ost")
        G = P.sb([128, 4 * KC], "gains_sb")
        CW = P.sb([128, 2 * NF * 3], "convw_sb")
        ones = P.sb([128, 128], "ones", dtype=FR)
        rstd = P.sb([128, nt], "rstd")
        tmpA = [P.sb([128, nt], f"tmpA{i}") for i in range(2)]
        sqr = [P.sb([128, nt], f"sqr{i}", dtype=FR) for i in range(2)]
        tmpB = [P.sb([128, nt], f"tmpB{i}") for i in range(2)]
        tmpC = [P.sb([128, nt], f"tmpC{i}") for i in range(2)]
        BIG = P.sb([128, NF, nt], "BIG", dtype=FR)
        PS = [P.ps(f"ps{i}") for i in range(8)]
        psi = [0]
        wri = [0]

        def next_ps():
            p = PS[psi[0] % 6]
            psi[0] += 1
            return p

        SSP = [PS[6], PS[7]]

        def wmm(ps, wsrc, nkc, rhs_obj, rhs_fn):
            for p0 in range(0, nkc, PK):
                npc = min(PK, nkc - p0)
                st_, wf = WS[wri[0] % 2], WF[wri[0] % 2]
                wri[0] += 1
                P.dma(st_[:, :npc * 128], wsrc[:, p0 * 128:(p0 + npc) * 128], writes=(st_,))
                P.issue("vector", lambda e, st_=st_, wf=wf, npc=npc: e.tensor_copy(out=wf[:, :npc * 128], in_=st_[:, :npc * 128]),
                        reads=(st_,), writes=(wf,))
                for j in range(npc):
                    mm(P, ps, ps[:, :nt], wf, wf[:, j * 128:(j + 1) * 128], rhs_obj, rhs_fn(p0 + j),
                       p0 + j == 0, p0 + j == nkc - 1)

        ones32 = P.sb([128, 128], "ones32")
        P.issue("vector", lambda e: e.memset(ones32[:], 1.0), writes=(ones32,))
        P.issue("vector", lambda e: e.tensor_copy(out=ones[:], in_=ones32[:]), reads=(ones32,), writes=(ones,))
        zer2 = P.sb([128, 2], "zer2")
        P.issue("vector", lambda e: e.memset(zer2[:], 0.0), writes=(zer2,))
        P.dma(G[:], gains, writes=(G,))
        P.dma(CW[:], convw, writes=(CW,))

        def norm_stats(src, ssp):
            for kc in range(KC):
                sq = sqr[kc % 2]
                act(P, sq, sq[:, :], src, src[:, kc, :], AF.Square)
                mm(P, ssp, ssp[:, :nt], ones, ones[:, :], sq, sq[:, :], kc == 0, kc == KC - 1)

        for s in range(nsh):
            P.dma(X[:], xT[s].rearrange("(kc p) t -> p kc t", p=128), writes=(X,))
            norm_stats(X, SSP[0])
            rstd_from_ss(P, rstd, SSP[0], D, nt)
            for kc in range(KC):
                stt(P, H, H[:, kc, :], X, X[:, kc, :], G[:, kc:kc + 1], rstd, rstd[:, :], ALU.mult, ALU.mult,
                    extra_reads=(G,))
            for n in range(4):
                On_lo = KC + 8 * (n % 2)
                P.dma(OST[:], oT[s, n * 1024:(n + 1) * 1024, :].rearrange("(kc p) t -> p kc t", p=128), writes=(OST,))
                P.issue("gpsimd", lambda e, On_lo=On_lo: e.tensor_copy(out=BIG[:, On_lo:On_lo + 8, :], in_=OST[:]),
                        reads=(OST,), writes=(BIG,))
                for dt in range(KC):
                    gps = next_ps()
                    wmm(gps, wg[n * KC + dt], KC, H, lambda kc: H[:, kc, :])
                    yps = next_ps()
                    wmm(yps, wb[n * KC + dt], 8, BIG, lambda kc, On_lo=On_lo: BIG[:, On_lo + kc, :])
                    sig = tmpB[dt % 2]
                    act(P, sig, sig[:, :], gps, gps[:, :nt], AF.Sigmoid)
                    if n == 0:
                        tt(P, BIG, BIG[:, dt, :], sig, sig[:, :], yps, yps[:, :nt], ALU.mult)
                    else:
                        tm = tmpC[dt % 2]
                        tt(P, tm, tm[:, :], sig, sig[:, :], yps, yps[:, :nt], ALU.mult)
                        tt(P, BIG, BIG[:, dt, :], BIG, BIG[:, dt, :].bitcast(F32), tm, tm[:, :], ALU.add)
            ssp = SSP[1]
            for dt in range(KC):
                zps = next_ps()
                wmm(zps, wo[dt], KC, BIG, lambda kc: BIG[:, kc, :])
                act(P, H, H[:, dt, :], zps, zps[:, :nt], AF.Copy)
                sq = sqr[dt % 2]
                act(P, sq, sq[:, :], zps, zps[:, :nt], AF.Square)
                mm(P, ssp, ssp[:, :nt], ones, ones[:, :], sq, sq[:, :], dt == 0, dt == KC - 1)
            rstd_from_ss(P, rstd, ssp, D, nt)
            for kc in range(KC):
                tm = tmpC[kc % 2]
                stt(P, tm, tm[:, :], H, H[:, kc, :].bitcast(F32), G[:, KC + kc:KC + kc + 1], rstd, rstd[:, :], ALU.mult, ALU.mult,
                    extra_reads=(G,))
                tt(P, X, X[:, kc, :], X, X[:, kc, :], tm, tm[:, :], ALU.add)
            norm_stats(X, SSP[0])
            rstd_from_ss(P, rstd, SSP[0], D, nt)
            for kc in range(KC):
                stt(P, H, H[:, kc, :], X, X[:, kc, :], G[:, 2 * KC + kc:2 * KC + kc + 1], rstd, rstd[:, :],
                    ALU.mult, ALU.mult, extra_reads=(G,))
            for f in range(NF):
                res = []
                for half in range(2):
                    ft = half * NF + f
                    ups = next_ps()
                    wmm(ups, wu[ft], KC, H, lambda kc: H[:, kc, :])
                    u = tmpA[half] if half == 0 else tmpB[0]
                    act(P, u, u[:, :], ups, ups[:, :nt], AF.Copy)
                    c = tmpC[half]
                    ts(P, c, c[:, 2:nt], u, u[:, 2:nt], CW[:, ft * 3 + 2:ft * 3 + 3], ALU.mult, extra_reads=(CW,))
                    stt(P, c, c[:, 2:nt], u, u[:, 1:nt - 1], CW[:, ft * 3 + 1:ft * 3 + 2], c, c[:, 2:nt],
                        ALU.mult, ALU.add, extra_reads=(CW,))
                    stt(P, c, c[:, 2:nt], u, u[:, 0:nt - 2], CW[:, ft * 3:ft * 3 + 1], c, c[:, 2:nt],
                        ALU.mult, ALU.add, extra_reads=(CW,))
                    res.append(c)
                sl = tmpB[1]
                act(P, sl, sl[:, 2:nt], res[0], res[0][:, 2:nt], AF.Silu)
                tt(P, BIG, BIG[:, f, 2:nt], sl, sl[:, 2:nt], res[1], res[1][:, 2:nt], ALU.mult)
                if s == 0:
                    P.issue("gpsimd", lambda e, f=f: e.tensor_copy(out=BIG[:, f, 0:2], in_=zer2[:, 0:2]), reads=(zer2,), writes=(BIG,))
            ssp = SSP[1]
            for dt in range(KC):
                zps = next_ps()
                wmm(zps, wd[dt], NF, BIG, lambda f: BIG[:, f, :])
                act(P, H, H[:, dt, :], zps, zps[:, :nt], AF.Copy)
                sq = sqr[dt % 2]
                act(P, sq, sq[:, :], zps, zps[:, :nt], AF.Square)
                mm(P, ssp, ssp[:, :nt], ones, ones[:, :], sq, sq[:, :], dt == 0, dt == KC - 1)
            rstd_from_ss(P, rstd, ssp, D, nt)
            for kc in range(KC):
                tm = tmpC[kc % 2]
                stt(P, tm, tm[:, :], H, H[:, kc, :].bitcast(F32), G[:, 3 * KC + kc:3 * KC + kc + 1], rstd, rstd[:, :],
                    ALU.mult, ALU.mult, extra_reads=(G,))
                tt(P, X, X[:, kc, :], X, X[:, kc, :], tm, tm[:, :], ALU.add)
            P.dma(yT[s].rearrange("(kc p) t -> p kc t", p=128), X[:], reads=(X,))
        P.emit()
    return nc


def tile_w(w, kc_inner=True):
    K, N = w.shape
    return np.ascontiguousarray(w.reshape(K // 128, 128, N // 128, 128).transpose(2, 1, 0, 3)).reshape(N // 128, 128, K)


def dense_weights(i, w_branch, w_gate, w_out, w_up, w_down, ffn_conv, pre_mix, post_mix, pre_ffn, post_ffn):
    KC = D // 128
    wgt = tile_w(w_gate[i])
    wbt = np.concatenate([tile_w(w_branch[i, n]) for n in range(4)], axis=0)
    wot = tile_w(w_out[i])
    wut = tile_w(w_up[i])
    wdt = tile_w(w_down[i])
    g = np.stack([pre_mix[i], post_mix[i], pre_ffn[i], post_ffn[i]], 0).reshape(4, KC, 128).transpose(2, 0, 1)
    g = np.ascontiguousarray(g).reshape(128, 4 * KC)
    cw = np.ascontiguousarray(ffn_conv[i].reshape(3, 2 * DFF // 128, 128).transpose(2, 1, 0)).reshape(128, -1)
    return dict(wg=wgt, wb=wbt, wo=wot, wu=wut, wd=wdt, gains=g.astype(np.float32), convw=cw.astype(np.float32))


def run_dense(xT_full, oT_full, wts, nsh_total=16):
    B, _, L = xT_full.shape
    ns = L // nsh_total
    nt = ns + 2
    nt += nt & 1
    per_core = B * nsh_total // NCORES
    xp = np.concatenate([np.zeros((B, D, 2), np.float32), xT_full, np.zeros((B, D, 2), np.float32)], axis=2)
    op = np.concatenate([np.zeros((B, D, 2), np.float32), oT_full, np.zeros((B, D, 2), np.float32)], axis=2)
    in_maps = []
    for c in range(NCORES):
        xs, os_ = [], []
        for j in range(per_core):
            g = c * per_core + j
            b, k = divmod(g, nsh_total)
            xs.append(xp[b, :, k * ns:k * ns + nt])
            os_.append(op[b, :, k * ns:k * ns + nt])
        m = dict(wts)
        m["xT"] = np.ascontiguousarray(np.stack(xs, 0))
        m["oT"] = np.ascontiguousarray(np.stack(os_, 0))
        in_maps.append(m)
    nc = build_dense(per_core, nt)
    res = run_bass_kernel_spmd(nc, in_maps, core_ids=list(range(NCORES)), **({"trace": True} if TRACE else {}))
    LAST_NS.append(res.exec_time_ns)
    out = np.zeros((B, D, L), np.float32)
    for c in range(NCORES):
        y = res.results[c]["yT"]
        for j in range(per_core):
            g = c * per_core + j
            b, k = divmod(g, nsh_total)
            out[b, :, k * ns:(k + 1) * ns] = y[j][:, 2:2 + ns]
    return out


ENG_A = "vector"
NT_IN = 34
PADL = 4
RQ, RK, RV, RG = 0, 2, 4, 6
AQ, AK, AV, AG, ALR = 8, 9, 10, 12, 14
CQ, CK, CV, CZ, CAB = 15, 17, 19, 21, 23
DR, DK, DV, DW, DA, DG = 24, 26, 28, 30, 31, 32


def build_mixer(T):
    nc = bass.Bass("TRN2", target_bir_lowering=False)
    KC = D // 128
    TB = 512
    blocks = [(t0, min(TB, T - t0)) for t0 in range(0, T, TB)]
    xT = nc.dram_tensor("xT", [D, T], F32, kind="ExternalInput").ap()
    win = nc.dram_tensor("win", [NT_IN, 128, KC * 128], F32, kind="ExternalInput").ap()
    gpre = nc.dram_tensor("gpre", [128, KC], F32, kind="ExternalInput").ap()
    rope = nc.dram_tensor("rope", [2, 128, T], F32, kind="ExternalInput").ap()
    pv = nc.dram_tensor("pv", [128, 64], F32, kind="ExternalInput").ap()
    gw2 = nc.dram_tensor("gw2", [16, 128], F32, kind="ExternalInput").ap()
    rw2 = nc.dram_tensor("rw2", [64, 256], F32, kind="ExternalInput").ap()
    ra2 = nc.dram_tensor("ra2", [64, 256], F32, kind="ExternalInput").ap()
    rg2 = nc.dram_tensor("rg2", [256, 256], F32, kind="ExternalInput").ap()
    consts = nc.dram_tensor("consts", [3, 128, 128], F32, kind="ExternalInput").ap()
    outT = nc.dram_tensor("outT", [1024, T], F32, kind="ExternalOutput").ap()
    PV_GAM, PV_RETN, PV_GLAB, PV_GLAN, PV_CONV, PV_ALOG, PV_DTB, PV_GDNN = 0, 1, 3, 4, 6, 30, 31, 32
    PV_MU, PV_W0, PV_A0, PV_KK, PV_KA, PV_RK, PV_LNW, PV_LNB = 33, 43, 45, 47, 49, 51, 53, 55

    with ExitStack() as stack:
        P = Prog(nc, stack)
        P.epsc = make_eps(P)
        PR = P.dram([NT_IN * 128, PADL + T], "PR")
        BRr = P.dram([T, 1, 512], "BRr")
        BRa = P.dram([T, 1, 384], "BRa")
        BRc = P.dram([T, 1, 1280], "BRc")
        BRd = P.dram([T, 2, 640], "BRd")
        VTc = P.dram([128, 2, T], "VTc")
        VTd = P.dram([128, 2, T], "VTd")
        OS = [P.dram([128, 2, T], f"OS{i}") for i in range(4)]
        BON = P.dram([256, T], "BON")
        GT = P.dram([256, T], "GT")
        OUT = Obj(outT, "outT")

        PVs = P.sb([128, 64], "pv_sb")
        CN = P.sb([128, 3, 128], "consts_sb")
        Gp = P.sb([128, KC], "gpre_sb")
        PS = [P.ps(f"ps{i}") for i in range(8)]
        psi = [0]

        def next_ps():
            p = PS[psi[0] % 7]
            psi[0] += 1
            return p
        SSP = PS[7]
        P.dma(PVs[:], pv, writes=(PVs,))
        P.dma(CN[:], consts.rearrange("a p c -> p a c"), writes=(CN,))
        P.dma(Gp[:], gpre, writes=(Gp,))
        ident, ones, bones = CN[:, 0, :], CN[:, 1, :], CN[:, 2, :]
        act(P, PVs, PVs[:, 30:31], PVs, PVs[:, 30:31], AF.Exp)
        ts(P, PVs, PVs[:, 30:31], PVs, PVs[:, 30:31], -1.0, ALU.mult)

        def pcol(c):
            return PVs[:, c:c + 1]

        with ExitStack() as st1:
            P.stack = st1
            X = P.sb([128, KC, TB], "X")
            H = P.sb([128, KC, TB], "H", dtype=FR)
            PK = 16
            WS = [P.sb([128, PK * 128], f"ws{i}") for i in range(2)]
            WF = [P.sb([128, PK * 128], f"wf{i}", dtype=FR) for i in range(2)]
            sqt = [P.sb([128, TB], f"sq{i}") for i in range(2)]
            ev = [P.sb([128, TB], f"ev{i}") for i in range(2)]
            rstd = P.sb([128, TB], "rstd")
            zt = P.sb([128, PADL], "zt")
            P.issue("vector", lambda e: e.memset(zt[:], 0.0), writes=(zt,))
            for ct in range(NT_IN):
                P.dma(PR[ct * 128:(ct + 1) * 128, 0:PADL], zt[:], reads=(zt,), writes=(PR,))
            wi = 0
            for (t0, tb) in blocks:
                P.dma(X[:, :, :tb], xT[:, t0:t0 + tb].rearrange("(kc p) t -> p kc t", p=128), writes=(X,))
                for kc in range(KC):
                    sq = sqt[kc % 2]
                    act(P, sq, sq[:, :tb], X, X[:, kc, :tb], AF.Square)
                    mm(P, SSP, SSP[:, :tb], CN, ones, sq, sq[:, :tb], kc == 0, kc == KC - 1)
                rstd_from_ss(P, rstd, SSP, D, tb)
                for kc in range(KC):
                    stt(P, H, H[:, kc, :tb], X, X[:, kc, :tb], Gp[:, kc:kc + 1], rstd, rstd[:, :tb],
                        ALU.mult, ALU.mult, extra_reads=(Gp,))
                for ct in range(NT_IN):
                    ps = next_ps()
                    for p0 in range(0, KC, PK):
                        st_, wf = WS[wi % 2], WF[wi % 2]
                        wi += 1
                        P.dma(st_[:], win[ct][:, p0 * 128:(p0 + PK) * 128], writes=(st_,))
                        P.issue("vector", lambda e, st_=st_, wf=wf: e.tensor_copy(out=wf[:], in_=st_[:]),
                                reads=(st_,), writes=(wf,))
                        for j in range(PK):
                            mm(P, ps, ps[:, :tb], wf, wf[:, j * 128:(j + 1) * 128], H, H[:, p0 + j, :tb],
                               p0 + j == 0, p0 + j == KC - 1)
                    e_ = ev[ct % 2]
                    act(P, e_, e_[:, :tb], ps, ps[:, :tb], AF.Copy)
                    P.dma(PR[ct * 128:(ct + 1) * 128, PADL + t0:PADL + t0 + tb], e_[:, :tb], reads=(e_,), writes=(PR,))
        P.stack = stack
        P.barrier()

        with ExitStack() as st2:
            P.stack = st2
            NTMP = 14
            tp = [P.sb([128, TB + PADL], f"tp{i}") for i in range(NTMP)]
            rows = [P.sb([128, 128], f"rows{i}") for i in range(4)]
            P_rows5 = P.sb([128, 5, 128], "rows5")
            ri = [0]

            def load(dst, tile_idx, t0, tb, halo=0, nrows=128):
                P.dma(dst[:nrows, :tb + halo],
                      PR[tile_idx * 128:tile_idx * 128 + nrows, PADL + t0 - halo:PADL + t0 + tb], reads=(PR,), writes=(dst,))

            def to_rows(src, tb, dst_obj, dst_fn):
                for s0 in range(0, tb, 128):
                    n = min(128, tb - s0)
                    ps = next_ps()
                    P.issue("tensor", lambda e, ps=ps, s0=s0, n=n: e.transpose(ps[:n, :128], src[:, s0:s0 + n], ident),
                            reads=(src, CN), writes=(ps,))
                    r = rows[ri[0] % 4]
                    ri[0] += 1
                    act(P, r, r[:n, :], ps, ps[:n, :128], AF.Copy)
                    yield r, s0, n

            def rows_out(src, tb, dst, t0, ph, off, width=128, c0=0):
                for r, s0, n in to_rows(src, tb, dst, None):
                    P.dma(dst[t0 + s0:t0 + s0 + n, ph, off:off + width], r[:n, c0:c0 + width], reads=(r,), writes=(dst,))

            def group_sum(dst, srcs, tb, which):
                for i, s_ in enumerate(srcs):
                    mm(P, dst, dst[:, :tb], CN, which, s_, s_[:, :tb], i == 0, i == len(srcs) - 1)

            for (t0, tb) in blocks:
                q1, q2, k1, k2, cs, sn, a_, b_, c_, d_ = tp[:10]
                load(q1, RQ, t0, tb); load(q2, RQ + 1, t0, tb); load(k1, RK, t0, tb); load(k2, RK + 1, t0, tb)
                P.dma(cs[:, :tb], rope[0, :, t0:t0 + tb], writes=(cs,))
                P.dma(sn[:, :tb], rope[1, :, t0:t0 + tb], writes=(sn,))
                for (x1, x2, off, scl) in ((k1, k2, 0, 1.0), (q1, q2, 256, 256 ** -0.5)):
                    tt(P, a_, a_[:, :tb], x1, x1[:, :tb], cs, cs[:, :tb], ALU.mult)
                    tt(P, b_, b_[:, :tb], x2, x2[:, :tb], sn, sn[:, :tb], ALU.mult)
                    tt(P, a_, a_[:, :tb], a_, a_[:, :tb], b_, b_[:, :tb], ALU.subtract)
                    tt(P, c_, c_[:, :tb], x1, x1[:, :tb], sn, sn[:, :tb], ALU.mult)
                    tt(P, d_, d_[:, :tb], x2, x2[:, :tb], cs, cs[:, :tb], ALU.mult)
                    tt(P, c_, c_[:, :tb], c_, c_[:, :tb], d_, d_[:, :tb], ALU.add)
                    if scl != 1.0:
                        ts(P, a_, a_[:, :tb], a_, a_[:, :tb], scl, ALU.mult)
                        ts(P, c_, c_[:, :tb], c_, c_[:, :tb], scl, ALU.mult)
                    rows_out(a_, tb, BRr, t0, 0, off)
                    rows_out(c_, tb, BRr, t0, 0, off + 128)
                lr, al, kk_, qq_ = tp[:4]
                load(lr, ALR, t0, tb, nrows=16)
                g2t = tp[4]
                P.dma(g2t[:16, :128], gw2, writes=(g2t,))
                ps = next_ps()
                mm(P, ps, ps[:, :tb], g2t, g2t[:16, :128], lr, lr[:16, :tb], True, True)
                nb_ = tp[5]
                ts(P, nb_, nb_[:, 0:1], PVs, pcol(PV_GLAB), -1.0, ALU.mult)
                act(P, al, al[:, :tb], ps, ps[:, :tb], AF.Exp, scale=-1.0, bias=nb_[:, 0:1], extra_reads=(nb_,))
                act(P, al, al[:, :tb], al, al[:, :tb], AF.Ln, scale=1.0, bias=P.epsc[:, 2:3], extra_reads=(P.epsc,))
                act(P, al, al[:, :tb], al, al[:, :tb], AF.Exp, scale=-1.0 / 16.0)
                rows_out(al, tb, BRa, t0, 0, 0)
                load(kk_, AK, t0, tb)
                rows_out(kk_, tb, BRa, t0, 0, 128)
                load(qq_, AQ, t0, tb)
                ts(P, qq_, qq_[:, :tb], qq_, qq_[:, :tb], 128 ** -0.5, ALU.mult)
                rows_out(qq_, tb, BRa, t0, 0, 256)
                ab, eg = tp[0], tp[1]
                load(ab, CAB, t0, tb)
                act(P, eg, eg[:, :tb], ab, ab[:, :tb], AF.Exp, scale=1.0, bias=pcol(PV_DTB), extra_reads=(PVs,))
                act(P, eg, eg[:, :tb], eg, eg[:, :tb], AF.Ln, scale=1.0, bias=P.epsc[:, 2:3], extra_reads=(P.epsc,))
                ts(P, eg, eg[:, :tb], eg, eg[:, :tb], pcol(PV_ALOG), ALU.mult, extra_reads=(PVs,))
                act(P, eg, eg[:, :tb], eg, eg[:, :tb], AF.Exp)
                bt = tp[2]
                act(P, bt, bt[:, :tb], ab, ab[:, :tb], AF.Sigmoid)
                egr = [None] * 8
                cv = {}
                for nm, base, pvoff in (("q", CQ, 0), ("k", CK, 2), ("v", CV, 4)):
                    for hh in range(2):
                        src = tp[3]
                        load(src, base + hh, t0, tb, halo=3)
                        dst = tp[4 + len(cv)]
                        wc = PV_CONV + (pvoff + hh) * 4
                        ts(P, dst, dst[:, :tb], src, src[:, 3:3 + tb], pcol(wc + 3), ALU.mult, extra_reads=(PVs,))
                        for j in range(3):
                            stt(P, dst, dst[:, :tb], src, src[:, j:j + tb], pcol(wc + j), dst, dst[:, :tb],
                                ALU.mult, ALU.add, extra_reads=(PVs,))
                        act(P, dst, dst[:, :tb], dst, dst[:, :tb], AF.Silu)
                        cv[(nm, hh)] = dst
                sqq = tp[10]
                rn = tp[11]
                for nm, scl in (("q", 128 ** -0.5), ("k", 1.0)):
                    for hh in range(2):
                        x_ = cv[(nm, hh)]
                        tt(P, sqq, sqq[:, :tb], x_, x_[:, :tb], x_, x_[:, :tb], ALU.mult)
                        ps = next_ps()
                        group_sum(ps, [sqq], tb, ones)
                        rsqrt(P, rn, rn[:, :tb], ps, ps[:, :tb], 1.0, P.epsc[:, 0:1], P.epsc)
                        if scl != 1.0:
                            ts(P, rn, rn[:, :tb], rn, rn[:, :tb], scl, ALU.mult)
                        tt(P, x_, x_[:, :tb], x_, x_[:, :tb], rn, rn[:, :tb], ALU.mult)
                for hh in range(2):
                    v_ = cv[("v", hh)]
                    P.dma(VTc[:, hh, t0:t0 + tb], v_[:, :tb], reads=(v_,), writes=(VTc,))
                for s0 in range(0, tb, 128):
                    n = min(128, tb - s0)
                    pe = next_ps()
                    P.issue("tensor", lambda e, pe=pe, s0=s0, n=n: e.transpose(pe[:n, :128], eg[:, s0:s0 + n], ident),
                            reads=(eg, CN), writes=(pe,))
                    pb = next_ps()
                    P.issue("tensor", lambda e, pb=pb, s0=s0, n=n: e.transpose(pb[:n, :128], bt[:, s0:s0 + n], ident),
                            reads=(bt, CN), writes=(pb,))
                    sc = rows[ri[0] % 4]; ri[0] += 1
                    act(P, sc, sc[:n, 0:4], pe, pe[:n, 0:4], AF.Copy)
                    P.issue("scalar", lambda e, sc=sc, pb=pb, n=n: e.activation(out=sc[:n, 2:4], in_=pb[:n, 2:4], func=AF.Copy),
                            reads=(pb,), writes=(sc,))
                    for hh in range(2):
                        pk = next_ps()
                        kx = cv[("k", hh)]
                        P.issue("tensor", lambda e, pk=pk, kx=kx, s0=s0, n=n: e.transpose(pk[:n, :128], kx[:, s0:s0 + n], ident),
                                reads=(kx, CN), writes=(pk,))
                        r5 = P_rows5
                        ts(P, r5, r5[:n, 0, :], CN, ones[:n, :], sc[:n, hh:hh + 1], ALU.mult, extra_reads=(sc,))
                        act(P, r5, r5[:n, 1, :], pk, pk[:n, :128], AF.Copy)
                        ts(P, r5, r5[:n, 3, :], pk, pk[:n, :128], sc[:n, 2 + hh:3 + hh], ALU.mult, extra_reads=(sc,))
                        ts(P, r5, r5[:n, 2, :], r5, r5[:n, 3, :], sc[:n, hh:hh + 1], ALU.mult, extra_reads=(sc,))
                        ts(P, r5, r5[:n, 2, :], r5, r5[:n, 2, :], -1.0, ALU.mult)
                        pq = next_ps()
                        qx = cv[("q", hh)]
                        P.issue("tensor", lambda e, pq=pq, qx=qx, s0=s0, n=n: e.transpose(pq[:n, :128], qx[:, s0:s0 + n], ident),
                                reads=(qx, CN), writes=(pq,))
                        act(P, r5, r5[:n, 4, :], pq, pq[:n, :128], AF.Copy)
                        P.dma(BRc[t0 + s0:t0 + s0 + n, 0, :].rearrange("t (v g k) -> t v g k", v=5, g=2)[:, :, hh, :],
                              r5[:n, :, :], reads=(r5,), writes=(BRc,))
                sh = {}
                for nm, base, ntile, mu0 in (("r", DR, 2, 0), ("k", DK, 2, 2), ("v", DV, 2, 4), ("w", DW, 1, 6),
                                             ("a", DA, 1, 7), ("g", DG, 2, 8)):
                    for i in range(ntile):
                        src = tp[13]
                        load(src, base + i, t0, tb, halo=1)
                        dst = tp[len(sh)]
                        tt(P, dst, dst[:, :tb], src, src[:, 0:tb], src, src[:, 1:1 + tb], ALU.subtract)
                        stt(P, dst, dst[:, :tb], dst, dst[:, :tb], pcol(PV_MU + mu0 + i), src, src[:, 1:1 + tb],
                            ALU.mult, ALU.add, extra_reads=(PVs,))
                        sh[(nm, i)] = dst
                wt_ = tp[10]
                lowr = tp[11]
                act(P, sh[("w", 0)], sh[("w", 0)][:, :tb], sh[("w", 0)], sh[("w", 0)][:, :tb], AF.Tanh)
                for i in range(2):
                    act(P, sh[("g", i)], sh[("g", i)][:, :tb], sh[("g", i)], sh[("g", i)][:, :tb], AF.Sigmoid)
                for i in range(2):
                    r_, k_, v_ = sh[("r", i)], sh[("k", i)], sh[("v", i)]
                    P.dma(VTd[:, i, t0:t0 + tb], v_[:, :tb], reads=(v_,), writes=(VTd,))
                    P.dma(lowr[:64, :128], rw2[:, i * 128:(i + 1) * 128], writes=(lowr,))
                    ps = next_ps()
                    mm(P, ps, ps[:, :tb], lowr, lowr[:64, :128], sh[("w", 0)], sh[("w", 0)][:64, :tb], True, True)
                    dec = tp[12]
                    act(P, dec, dec[:, :tb], ps, ps[:, :tb], AF.Sigmoid, bias=pcol(PV_W0 + i), extra_reads=(PVs,))
                    act(P, dec, dec[:, :tb], dec, dec[:, :tb], AF.Exp, scale=-0.606531)
                    for hh in range(2):
                        rows_out(dec, tb, BRd, t0, hh, (0 * 2 + i) * 64, width=64, c0=hh * 64)
                    P.dma(lowr[:64, :128], ra2[:, i * 128:(i + 1) * 128], writes=(lowr,))
                    ps = next_ps()
                    mm(P, ps, ps[:, :tb], lowr, lowr[:64, :128], sh[("a", 0)], sh[("a", 0)][:64, :tb], True, True)
                    aa = tp[12]
                    act(P, aa, aa[:, :tb], ps, ps[:, :tb], AF.Sigmoid, bias=pcol(PV_A0 + i), extra_reads=(PVs,))
                    ts(P, wt_, wt_[:, :tb], k_, k_[:, :tb], pcol(PV_KK + i), ALU.mult, extra_reads=(PVs,))
                    sq_ = tp[13]
                    tt(P, sq_, sq_[:, :tb], wt_, wt_[:, :tb], wt_, wt_[:, :tb], ALU.mult)
                    ps = next_ps()
                    group_sum(ps, [sq_], tb, bones)
                    rsqrt(P, sq_, sq_[:, :tb], ps, ps[:, :tb], 1.0, P.epsc[:, 0:1], P.epsc)
                    tt(P, wt_, wt_[:, :tb], wt_, wt_[:, :tb], sq_, sq_[:, :tb], ALU.mult)
                    for hh in range(2):
                        rows_out(wt_, tb, BRd, t0, hh, (1 * 2 + i) * 64, width=64, c0=hh * 64)
                    tt(P, wt_, wt_[:, :tb], wt_, wt_[:, :tb], aa, aa[:, :tb], ALU.mult)
                    ts(P, wt_, wt_[:, :tb], wt_, wt_[:, :tb], -1.0, ALU.mult)
                    for hh in range(2):
                        rows_out(wt_, tb, BRd, t0, hh, (2 * 2 + i) * 64, width=64, c0=hh * 64)
                    ts(P, aa, aa[:, :tb], aa, aa[:, :tb], -1.0, ALU.add)
                    ts(P, aa, aa[:, :tb], aa, aa[:, :tb], pcol(PV_KA + i), ALU.mult, extra_reads=(PVs,))
                    ts(P, aa, aa[:, :tb], aa, aa[:, :tb], 1.0, ALU.add)
                    tt(P, k_, k_[:, :tb], k_, k_[:, :tb], aa, aa[:, :tb], ALU.mult)
                    for hh in range(2):
                        rows_out(k_, tb, BRd, t0, hh, (3 * 2 + i) * 64, width=64, c0=hh * 64)
                        rows_out(r_, tb, BRd, t0, hh, (4 * 2 + i) * 64, width=64, c0=hh * 64)
                    tt(P, sq_, sq_[:, :tb], r_, r_[:, :tb], k_, k_[:, :tb], ALU.mult)
                    ts(P, sq_, sq_[:, :tb], sq_, sq_[:, :tb], pcol(PV_RK + i), ALU.mult, extra_reads=(PVs,))
                    ps = next_ps()
                    group_sum(ps, [sq_], tb, bones)
                    tt(P, sq_, sq_[:, :tb], ps, ps[:, :tb], v_, v_[:, :tb], ALU.mult)
                    P.dma(BON[i * 128:(i + 1) * 128, t0:t0 + tb], sq_[:, :tb], reads=(sq_,), writes=(BON,))
                    ps = next_ps()
                    P.dma(lowr[:, :128], rg2[0:128, i * 128:(i + 1) * 128], writes=(lowr,))
                    mm(P, ps, ps[:, :tb], lowr, lowr[:, :128], sh[("g", 0)], sh[("g", 0)][:, :tb], True, False)
                    lowr2 = tp[12]
                    P.dma(lowr2[:, :128], rg2[128:256, i * 128:(i + 1) * 128], writes=(lowr2,))
                    mm(P, ps, ps[:, :tb], lowr2, lowr2[:, :128], sh[("g", 1)], sh[("g", 1)][:, :tb], False, True)
                    act(P, sq_, sq_[:, :tb], ps, ps[:, :tb], AF.Copy)
                    P.dma(GT[i * 128:(i + 1) * 128, t0:t0 + tb], sq_[:, :tb], reads=(sq_,), writes=(GT,))
        P.stack = stack

        P.barrier()
        st3 = ExitStack()
        P.stack = st3

        def scan(eng, dq, TC, BR, PH, Wd, vsrc_fn, OD, G, K, vecs, lowrank, shared, gam=None):
            S = P.sb([128, G, K], P.uid("S"))
            tmp = P.sb([128, G, K], P.uid("stmp"))
            sa = P.sb([128, G], P.uid("sa"))
            RB = [P.sb([128, TC, Wd], P.uid("rb")) for _ in range(2)]
            VB = [P.sb([128, G, TC], P.uid("vb")) for _ in range(2)]
            OB = [P.sb([128, G, TC], P.uid("ob")) for _ in range(2)]
            TQ = [P.sb([128, G, K], P.uid("tq")) for _ in range(4)] if eng != "vector" else None
            junk = P.sb([128, K], P.uid("junk")) if eng != "vector" else None
            P.issue(eng, lambda e: e.memset(S[:], 0.0), writes=(S,))
            for ci, c0 in enumerate(range(0, T, TC)):
                n = min(TC, T - c0)
                rb, vb, ob = RB[ci % 2], VB[ci % 2], OB[ci % 2]
                for ph in range(PH):
                    np_ = 128 // PH
                    P.dma(rb[ph * np_:(ph + 1) * np_, :n, :],
                          BR[c0:c0 + n, ph, :].partition_broadcast(np_), reads=(BR,), writes=(rb,), queue=dq)
                vsrc_fn(vb, c0, n, dq)
                for i in range(n):
                    def row(nm):
                        v = vecs[nm]
                        if shared:
                            return rb[:, i, v * K:(v + 1) * K].unsqueeze(1).broadcast_to([128, G, K])
                        return rb[:, i, v * G * K:(v + 1) * G * K].rearrange("p (g k) -> p g k", g=G)

                    def rowg(nm, g):
                        v = vecs[nm]
                        if shared:
                            return rb[:, i, v * K:(v + 1) * K]
                        return rb[:, i, (v * G + g) * K:(v * G + g + 1) * K]
                    if lowrank:
                        tt(P, tmp, tmp[:], S, S[:], rb, row("kk"), ALU.mult, eng=eng)
                        yield
                        P.issue(eng, lambda e: e.tensor_reduce(out=sa[:], in_=tmp[:], axis=AX.X, op=ALU.add),
                                reads=(tmp,), writes=(sa,))
                        yield
                    if gam is not None:
                        ts(P, S, S[:], S, S[:], gam, ALU.mult, extra_reads=(PVs,), eng=eng)
                    else:
                        tt(P, S, S[:], S, S[:], rb, row("w"), ALU.mult, eng=eng)
                    yield
                    if lowrank:
                        for g in range(G):
                            stt(P, S, S[:, g, :], rb, rowg("nb", g), sa[:, g:g + 1], S, S[:, g, :], ALU.mult, ALU.add,
                                extra_reads=(sa,), eng=eng)
                            yield
                    for g in range(G):
                        stt(P, S, S[:, g, :], rb, rowg("kp", g), vb[:, g, i:i + 1], S, S[:, g, :], ALU.mult, ALU.add,
                            extra_reads=(vb,), eng=eng)
                        yield
                    tt(P, tmp, tmp[:], S, S[:], rb, row("q"), ALU.mult, eng=eng)
                    yield
                    P.issue(eng, lambda e, ob=ob, i=i: e.tensor_reduce(out=ob[:, :, i], in_=tmp[:], axis=AX.X, op=ALU.add),
                            reads=(tmp,), writes=(ob,))
                    yield
                P.dma(OD[:, :, c0:c0 + n], ob[:, :, :n], reads=(ob,), writes=(OD,), queue=dq)
                yield

        def v_from_pr(tile0):
            def f(vb, c0, n, dq):
                for g in range(2):
                    P.dma(vb[:, g, :n], PR[(tile0 + g) * 128:(tile0 + g + 1) * 128, PADL + c0:PADL + c0 + n],
                          reads=(PR,), writes=(vb,), queue=dq)
            return f

        def v_from_vtc(vb, c0, n, dq):
            P.dma(vb[:, :, :n], VTc[:, :, c0:c0 + n], reads=(VTc,), writes=(vb,), queue=dq)

        def v_from_vtd(vb, c0, n, dq):
            P.dma(vb[:, :, :n], VTd[:, :, c0:c0 + n], reads=(VTd,), writes=(vb,), queue=dq)

        DQA = "scalar" if ENG_A != "vector" else "sync"
        gens = [
            (scan(ENG_A, DQA, 8, BRr, 1, 512, v_from_pr(RV), OS[0], 2, 256, {"kp": 0, "q": 1}, False, True, gam=pcol(PV_GAM)), 1),
            (scan("vector", "sync", 4, BRc, 1, 1280, v_from_vtc, OS[2], 2, 128, {"w": 0, "kk": 1, "nb": 2, "kp": 3, "q": 4}, True, False), 2),
            (scan("vector", "sync", 8, BRa, 1, 384, v_from_pr(AV), OS[1], 2, 128, {"w": 0, "kp": 1, "q": 2}, False, True), 1),
            (scan("vector", "sync", 8, BRd, 2, 640, v_from_vtd, OS[3], 2, 64, {"w": 0, "kk": 1, "nb": 2, "kp": 3, "q": 4}, True, False), 1),
        ]
        alive = [not SKIP_SCANS] * len(gens)
        while any(alive):
            for gi, (gen, reps) in enumerate(gens):
                if alive[gi]:
                    try:
                        next(gen)
                    except StopIteration:
                        alive[gi] = False
        P.barrier()
        st3.close()
        P.stack = stack

        with ExitStack() as st4:
            P.stack = st4
            tq = [P.sb([128, TB], f"tq{i}") for i in range(10)]
            for (t0, tb) in blocks:
                for m in range(4):
                    o0, o1, g0, g1, sq0, sq1, mean, rn = tq[:8]
                    P.dma(o0[:, :tb], OS[m][:, 0, t0:t0 + tb], reads=(OS[m],), writes=(o0,))
                    P.dma(o1[:, :tb], OS[m][:, 1, t0:t0 + tb], reads=(OS[m],), writes=(o1,))
                    oo = [o0, o1]
                    if m in (0, 1):
                        groups, which, n_el = [[0, 1]], ones, 256.0
                    elif m == 2:
                        groups, which, n_el = [[0], [1]], ones, 128.0
                    else:
                        groups, which, n_el = [[0], [1]], bones, 64.0
                    center = m in (0, 3)
                    eps = 64e-5 if m == 3 else EPS
                    for grp in groups:
                        if center:
                            ps = next_ps()
                            group_sum(ps, [oo[g] for g in grp], tb, which)
                            ts(P, mean, mean[:, :tb], ps, ps[:, :tb], 1.0 / n_el, ALU.mult)
                            for g in grp:
                                tt(P, oo[g], oo[g][:, :tb], oo[g], oo[g][:, :tb], mean, mean[:, :tb], ALU.subtract)
                        sqs = [sq0, sq1]
                        for g in grp:
                            tt(P, sqs[g], sqs[g][:, :tb], oo[g], oo[g][:, :tb], oo[g], oo[g][:, :tb], ALU.mult)
                        ps = next_ps()
                        group_sum(ps, [sqs[g] for g in grp], tb, which)
                        rsqrt(P, rn, rn[:, :tb], ps, ps[:, :tb], 1.0 / n_el, P.epsc[:, 1:2] if m == 3 else P.epsc[:, 0:1], P.epsc)
                        for g in grp:
                            ncol = {0: PV_RETN + g, 1: PV_GLAN + g, 2: PV_GDNN, 3: PV_LNW + g}[m]
                            stt(P, oo[g], oo[g][:, :tb], oo[g], oo[g][:, :tb], pcol(ncol), rn, rn[:, :tb],
                                ALU.mult, ALU.mult, extra_reads=(PVs,))
                    for g in range(2):
                        gt_ = [g0, g1][g]
                        if m == 3:
                            ts(P, oo[g], oo[g][:, :tb], oo[g], oo[g][:, :tb], pcol(PV_LNB + g), ALU.add, extra_reads=(PVs,))
                            P.dma(gt_[:, :tb], BON[g * 128:(g + 1) * 128, t0:t0 + tb], reads=(BON,), writes=(gt_,))
                            tt(P, oo[g], oo[g][:, :tb], oo[g], oo[g][:, :tb], gt_, gt_[:, :tb], ALU.add)
                            gt2 = tq[8]
                            P.dma(gt2[:, :tb], GT[g * 128:(g + 1) * 128, t0:t0 + tb], reads=(GT,), writes=(gt2,))
                            tt(P, oo[g], oo[g][:, :tb], oo[g], oo[g][:, :tb], gt2, gt2[:, :tb], ALU.mult)
                        else:
                            gtile = {0: RG, 1: AG, 2: CZ}[m] + g
                            P.dma(gt_[:, :tb], PR[gtile * 128:(gtile + 1) * 128, PADL + t0:PADL + t0 + tb],
                                  reads=(PR,), writes=(gt_,))
                            act(P, gt_, gt_[:, :tb], gt_, gt_[:, :tb], AF.Silu)
                            tt(P, oo[g], oo[g][:, :tb], oo[g], oo[g][:, :tb], gt_, gt_[:, :tb], ALU.mult)
                        P.dma(outT[m * 256 + g * 128:m * 256 + (g + 1) * 128, t0:t0 + tb], oo[g][:, :tb],
                              reads=(oo[g],), writes=(OUT,))
        P.stack = stack
        P.emit()
    return nc


def mixer_cols(jq):
    A_q, A_k, A_v, A_g = 0, 1024, 2048, 3072
    B_q, B_k, B_v, B_g, B_lr = 4096, 4608, 5120, 6144, 7168
    C_q, C_k, C_v, C_z, C_a, C_b = 7184, 8208, 9232, 10256, 11280, 11288
    Ds = 11296
    groups = [
        (np.arange(A_q + jq * 256, A_q + jq * 256 + 256), 256), (np.arange(A_k + jq * 256, A_k + jq * 256 + 256), 256),
        (np.arange(A_v + jq * 256, A_v + jq * 256 + 256), 256), (np.arange(A_g + jq * 256, A_g + jq * 256 + 256), 256),
        (np.arange(B_q + jq * 128, B_q + jq * 128 + 128), 128), (np.arange(B_k + jq * 128, B_k + jq * 128 + 128), 128),
        (np.arange(B_v + jq * 256, B_v + jq * 256 + 256), 256), (np.arange(B_g + jq * 256, B_g + jq * 256 + 256), 256),
        (np.arange(B_lr, B_lr + 16), 128),
        (np.arange(C_q + jq * 256, C_q + jq * 256 + 256), 256), (np.arange(C_k + jq * 256, C_k + jq * 256 + 256), 256),
        (np.arange(C_v + jq * 256, C_v + jq * 256 + 256), 256), (np.arange(C_z + jq * 256, C_z + jq * 256 + 256), 256),
        (np.array([C_a + 2 * jq, C_a + 2 * jq + 1, C_b + 2 * jq, C_b + 2 * jq + 1]), 128),
        (np.arange(Ds + jq * 256, Ds + jq * 256 + 256), 256), (np.arange(Ds + 1024 + jq * 256, Ds + 1024 + jq * 256 + 256), 256),
        (np.arange(Ds + 2048 + jq * 256, Ds + 2048 + jq * 256 + 256), 256),
        (np.arange(Ds + 3072, Ds + 3136), 128), (np.arange(Ds + 3136, Ds + 3200), 128), (np.arange(Ds + 3200, Ds + 3360), 256),
    ]
    return groups


def mixer_params(i, jq, T, p):
    groups = mixer_cols(jq)
    w_in = p["w_in"][i]
    W = np.zeros((D, NT_IN * 128), np.float32)
    off = 0
    for cols, width in groups:
        W[:, off:off + len(cols)] = w_in[:, cols]
        off += width
    assert off == NT_IN * 128
    pv = np.zeros((128, 64), np.float32)
    pp = np.arange(128)
    pv[:, 0] = 1.0 - 2.0 ** (-5.0 - jq)
    for g in range(2):
        pv[:, 1 + g] = p["ret_norm"][i][jq * 256 + g * 128 + pp]
        pv[:, 4 + g] = p["gla_norm"][i][jq * 256 + g * 128 + pp]
    pv[:, 3] = p["gla_b"][i][jq * 128 + pp]
    for pvoff, base in ((0, 0), (2, 1024), (4, 2048)):
        for hh in range(2):
            for j in range(4):
                pv[:, 6 + (pvoff + hh) * 4 + j] = p["gdn_conv"][i][j, base + (2 * jq + hh) * 128 + pp]
    pv[0:2, 30] = p["gdn_a_log"][i][2 * jq:2 * jq + 2]
    pv[0:2, 31] = p["gdn_dt_bias"][i][2 * jq:2 * jq + 2]
    pv[:, 32] = p["gdn_norm"][i]
    mu = p["rwkv_mu"][i]
    for t_ in range(2):
        pv[:, 33 + t_] = mu[0 + jq * 256 + t_ * 128 + pp]
        pv[:, 35 + t_] = mu[1024 + jq * 256 + t_ * 128 + pp]
        pv[:, 37 + t_] = mu[2048 + jq * 256 + t_ * 128 + pp]
    pv[:64, 39] = mu[3072:3136]
    pv[:64, 40] = mu[3136:3200]
    pv[:, 41] = mu[3200:3328]
    pv[:32, 42] = mu[3328:3360]
    for t_ in range(2):
        sl = jq * 256 + t_ * 128 + pp
        pv[:, 43 + t_] = p["rwkv_w0"][i][sl]
        pv[:, 45 + t_] = p["rwkv_a0"][i][sl]
        pv[:, 47 + t_] = p["rwkv_kk"][i][sl]
        pv[:, 49 + t_] = p["rwkv_ka"][i][sl]
        pv[:, 51 + t_] = p["rwkv_rk"][i].reshape(-1)[sl]
        pv[:, 53 + t_] = p["rwkv_ln_w"][i][sl]
        pv[:, 55 + t_] = p["rwkv_ln_b"][i][sl]
    rg2 = np.zeros((256, 256), np.float32)
    rg2[:160] = p["rwkv_g2"][i][:, jq * 256:(jq + 1) * 256]
    gpre = np.ascontiguousarray(p["pre_mix"][i].reshape(D // 128, 128).T)
    return dict(win=tile_w(W), pv=pv, gpre=gpre.astype(np.float32),
                gw2=np.ascontiguousarray(p["gla_w2"][i][:, jq * 128:(jq + 1) * 128]),
                rw2=np.ascontiguousarray(p["rwkv_w2"][i][:, jq * 256:(jq + 1) * 256]),
                ra2=np.ascontiguousarray(p["rwkv_a2"][i][:, jq * 256:(jq + 1) * 256]), rg2=rg2)


def const_tables(T):
    half = 128
    inv_freq = (10000.0 ** (-np.arange(half, dtype=np.float32) / half)).astype(np.float32)
    ang = (np.arange(T, dtype=np.float32)[None, :] * inv_freq[:, None]).astype(np.float32)
    rope = np.stack([np.cos(ang), np.sin(ang)], 0).astype(np.float32)
    bones = np.zeros((128, 128), np.float32)
    bones[:64, :64] = 1.0
    bones[64:, 64:] = 1.0
    consts = np.stack([np.eye(128, dtype=np.float32), np.ones((128, 128), np.float32), bones], 0)
    return rope, consts


def run_mixer(i, hT, p):
    B, _, T = hT.shape
    rope, consts = const_tables(T)
    in_maps = []
    for c in range(NCORES):
        b, jq = divmod(c, 4)
        m = mixer_params(i, jq, T, p)
        m["xT"] = np.ascontiguousarray(hT[b])
        m["rope"] = rope
        m["consts"] = consts
        in_maps.append(m)
    nc = build_mixer(T)
    res = run_bass_kernel_spmd(nc, in_maps, core_ids=list(range(NCORES)), **({"trace": True} if TRACE else {}))
    LAST_NS.append(res.exec_time_ns)
    oT = np.zeros((B, D, T), np.float32)
    for c in range(NCORES):
        b, jq = divmod(c, 4)
        o = res.results[c]["outT"]
        for m_ in range(4):
            oT[b, m_ * 1024 + jq * 256:m_ * 1024 + (jq + 1) * 256] = o[m_ * 256:(m_ + 1) * 256]
    return oT


def kernel(**inp):
    inp = {k: np.asarray(v) for k, v in inp.items()}
    x = inp["x"].astype(np.float32)
    B, S, _ = x.shape
    meta = np.broadcast_to(inp["meta"][None], (B, NMETA, D))
    h = np.concatenate([meta, x], axis=1)
    hT = np.ascontiguousarray(h.transpose(0, 2, 1))
    for i in range(2):
        oT = run_mixer(i, hT, inp)
        wts = dense_weights(i, inp["w_branch"], inp["w_gate"], inp["w_out"], inp["w_up"], inp["w_down"],
                            inp["ffn_conv"], inp["pre_mix"], inp["post_mix"], inp["pre_ffn"], inp["post_ffn"])
        hT = run_dense(hT, oT, wts)
    return np.ascontiguousarray(hT[:, :, NMETA:].transpose(0, 2, 1)).astype(np.float32)


def rstd_from_ss(P, rstd, ss_ps, n, nt, eps=EPS):
    if not hasattr(P, "epsc"):
        P.epsc = make_eps(P)
    rsqrt(P, rstd, rstd[:, :nt], ss_ps, ss_ps[:, :nt], 1.0 / n, P.epsc[:, 0:1], P.epsc)
```

```python
from contextlib import ExitStack
import numpy as np
import concourse.bass as bass
import concourse.mybir as mybir
from concourse.bass_utils import run_bass_kernel_spmd

F32 = mybir.dt.float32
FR = mybir.dt.float32r
ALU = mybir.AluOpType
AF = mybir.ActivationFunctionType
AX = mybir.AxisListType

D = 4096
DFF = 11008
NMETA = 16
EPS = 1e-6
NCORES = 8
TRACE = False
SKIP_SCANS = False
SKIP_PREP = False
LAST_NS = []


class Obj:
    def __init__(self, t, name):
        self.t = t
        self.name = name
        self.w = {}
        self.r = {}

    def __getitem__(self, k):
        return self.t[k]


class Prog:
    COMPUTE = ("scalar", "vector", "gpsimd", "tensor")
    NDS = 12

    def __init__(self, nc, stack):
        self.nc = nc
        self.stack = stack
        self.ops = {e: [] for e in self.COMPUTE + ("sync",)}
        self.semh = {}
        self.cnt = {}
        for e in self.COMPUTE:
            self.semh["s_" + e] = stack.enter_context(nc.semaphore("s_" + e))
            self.cnt[e] = 0
        self.dcnt = [0] * self.NDS
        self.dnext = 0
        for j in range(self.NDS):
            self.semh[f"d{j}"] = stack.enter_context(nc.semaphore(f"d{j}"))
        self.seen = {e: {} for e in self.ops}
        self.nuid = 0
        self.pending_waits = {}
        self.bar_tiles = {e: self.sb([128, 1], "bar_" + e) for e in ("scalar", "vector", "gpsimd")}

    def uid(self, p):
        self.nuid += 1
        return f"{p}{self.nuid}"

    def sb(self, shape, name=None, dtype=F32):
        name = name or self.uid("t")
        t = self.stack.enter_context(self.nc.sbuf_tensor(name, list(shape), dtype))
        return Obj(t, name)

    def ps(self, name=None):
        name = name or self.uid("p")
        t = self.stack.enter_context(self.nc.psum_tensor(name, [128, 512], F32))
        return Obj(t, name)

    def dram(self, shape, name=None):
        name = name or self.uid("dr")
        t = self.nc.dram_tensor(name, list(shape), F32, kind="Internal")
        return Obj(t.ap(), name)

    def _deps(self, reads, writes):
        waits = {}
        for t in reads:
            for k, v in t.w.items():
                waits[k] = max(waits.get(k, 0), v)
        for t in writes:
            for k, v in list(t.w.items()) + list(t.r.items()):
                waits[k] = max(waits.get(k, 0), v)
        return waits

    def _commit(self, eng, waits, fn, key, inc, val, reads, writes):
        seen = self.seen[eng]
        wl = []
        for k, v in waits.items():
            if seen.get(k, 0) < v:
                seen[k] = v
                wl.append((k, v))
        if self.pending_waits.get(eng):
            wl = self.pending_waits.pop(eng) + wl
        self.ops[eng].append((wl, fn, key, inc))
        for t in writes:
            t.w = {key: val}
            t.r = {}
        for t in reads:
            if t not in writes:
                t.r[key] = max(t.r.get(key, 0), val)

    def issue(self, eng, fn, reads=(), writes=()):
        waits = self._deps(reads, writes)
        key = "s_" + eng
        if eng == "tensor":
            waits.pop(key, None)
        self.cnt[eng] += 1
        self._commit(eng, waits, fn, key, 1, self.cnt[eng], reads, writes)

    def dma(self, out_ap, in_ap, reads=(), writes=(), queue="sync"):
        j = self.dnext
        self.dnext = (j + 1) % self.NDS
        key = f"d{j}"
        waits = self._deps(reads, writes)
        if self.dcnt[j] > 0:
            waits[key] = max(waits.get(key, 0), self.dcnt[j])
        self.dcnt[j] += 16
        self._commit(queue, waits, lambda e, o=out_ap, i=in_ap: e.dma_start(out=o, in_=i),
                     key, 16, self.dcnt[j], reads, writes)

    def barrier(self):
        bars = {}
        for e in self.COMPUTE:
            if not hasattr(self, "bar_tiles"):
                self.bar_tiles = {}
            if e not in self.bar_tiles and e != "tensor":
                self.bar_tiles[e] = self.sb([128, 1], "bar_" + e)
        allw = {"s_" + e: self.cnt[e] for e in self.COMPUTE if self.cnt[e] > 0}
        for j in range(self.NDS):
            if self.dcnt[j] > 0:
                allw[f"d{j}"] = self.dcnt[j]
        for e in ("scalar", "vector", "gpsimd"):
            t = self.bar_tiles[e]
            if e == "scalar":
                fn = lambda en, t=t: en.activation(out=t[:, :], in_=t[:, :], func=AF.Copy)
            else:
                fn = lambda en, t=t: en.memset(t[:, :], 0.0)
            self.cnt[e] += 1
            self._commit(e, dict(allw), fn, "s_" + e, 1, self.cnt[e], (), (t,))
        for e in ("tensor", "sync"):
            seen = self.seen[e]
            pend = [(k, v) for k, v in allw.items() if seen.get(k, 0) < v and k != "s_" + e]
            for k, v in pend:
                seen[k] = v
            self.pending_waits.setdefault(e, []).extend(pend)

    def emit(self):
        nc = self.nc
        finals = [(f"d{j}", self.dcnt[j]) for j in range(self.NDS) if self.dcnt[j] > 0]

        def run(e, name):
            for wl, fn, key, inc in self.ops[name]:
                for k, v in wl:
                    e.wait_ge(self.semh[k], v)
                fn(e).then_inc(self.semh[key], inc)
            if name == "sync":
                for k, v in finals:
                    e.wait_ge(self.semh[k], v)

        with nc.Block() as block:
            @block.sync
            def _(e):
                run(e, "sync")

            @block.scalar
            def _(e):
                run(e, "scalar")

            @block.vector
            def _(e):
                run(e, "vector")

            @block.gpsimd
            def _(e):
                run(e, "gpsimd")

            @block.tensor
            def _(e):
                run(e, "tensor")


def mm(P, out, out_ap, lhsT, lhsT_ap, rhs, rhs_ap, start, stop):
    P.issue("tensor", lambda e: e.matmul(out_ap, lhsT_ap, rhs_ap, start=start, stop=stop),
            reads=(lhsT, rhs), writes=(out,))


def act(P, out, out_ap, in_, in_ap, func, scale=1.0, bias=None, extra_reads=()):
    if bias is None:
        P.issue("scalar", lambda e: e.activation(out=out_ap, in_=in_ap, func=func, scale=scale),
                reads=(in_,) + tuple(extra_reads), writes=(out,))
    else:
        P.issue("scalar", lambda e: e.activation(out=out_ap, in_=in_ap, func=func, scale=scale, bias=bias),
                reads=(in_,) + tuple(extra_reads), writes=(out,))


def tt(P, out, out_ap, a, a_ap, b, b_ap, op, eng="vector"):
    P.issue(eng, lambda e: e.tensor_tensor(out=out_ap, in0=a_ap, in1=b_ap, op=op),
            reads=(a, b), writes=(out,))


def ts(P, out, out_ap, a, a_ap, s1, op0, s2=None, op1=None, extra_reads=(), eng="vector"):
    if op1 is None:
        P.issue(eng, lambda e: e.tensor_scalar(out=out_ap, in0=a_ap, scalar1=s1, scalar2=None, op0=op0),
                reads=(a,) + tuple(extra_reads), writes=(out,))
    else:
        P.issue(eng, lambda e: e.tensor_scalar(out=out_ap, in0=a_ap, scalar1=s1, scalar2=s2, op0=op0, op1=op1),
                reads=(a,) + tuple(extra_reads), writes=(out,))


def stt(P, out, out_ap, a, a_ap, scalar, b, b_ap, op0, op1, extra_reads=(), eng="vector"):
    P.issue(eng, lambda e: e.scalar_tensor_tensor(out=out_ap, in0=a_ap, scalar=scalar, in1=b_ap, op0=op0, op1=op1),
            reads=(a, b) + tuple(extra_reads), writes=(out,))


def rsqrt(P, out, out_ap, in_, in_ap, scale, eps_ap, eps_obj):
    act(P, out, out_ap, in_, in_ap, AF.Sqrt, scale=scale, bias=eps_ap, extra_reads=(eps_obj,))
    P.issue("vector", lambda e: e.reciprocal(out=out_ap, in_=out_ap), reads=(out,), writes=(out,))


def make_eps(P):
    t = P.sb([128, 4], P.uid("epsc"))
    P.issue("vector", lambda e: e.memset(t[:, 0:1], EPS), writes=(t,))
    P.issue("vector", lambda e: e.memset(t[:, 1:2], 64e-5), writes=(t,))
    P.issue("vector", lambda e: e.memset(t[:, 2:3], 1.0), writes=(t,))
    return t


def rstd_from_ss_old(P, rstd, ss_ps, n, nt, eps=EPS):
    ts(P, rstd, rstd[:, :nt], ss_ps, ss_ps[:, :nt], 1.0 / n, ALU.mult, eps, ALU.add)
    ts(P, rstd, rstd[:, :nt], rstd, rstd[:, :nt], -0.5, ALU.pow)


def build_dense(nsh, nt):
    nc = bass.Bass("TRN2", target_bir_lowering=False)
    KC = D // 128
    NF = DFF // 128
    xT = nc.dram_tensor("xT", [nsh, D, nt], F32, kind="ExternalInput").ap()
    oT = nc.dram_tensor("oT", [nsh, D, nt], F32, kind="ExternalInput").ap()
    wg = nc.dram_tensor("wg", [4 * KC, 128, KC * 128], F32, kind="ExternalInput").ap()
    wb = nc.dram_tensor("wb", [4 * KC, 128, 8 * 128], F32, kind="ExternalInput").ap()
    wo = nc.dram_tensor("wo", [KC, 128, KC * 128], F32, kind="ExternalInput").ap()
    wu = nc.dram_tensor("wu", [2 * NF, 128, KC * 128], F32, kind="ExternalInput").ap()
    wd = nc.dram_tensor("wd", [KC, 128, NF * 128], F32, kind="ExternalInput").ap()
    gains = nc.dram_tensor("gains", [128, 4 * KC], F32, kind="ExternalInput").ap()
    convw = nc.dram_tensor("convw", [128, 2 * NF * 3], F32, kind="ExternalInput").ap()
    yT = nc.dram_tensor("yT", [nsh, D, nt], F32, kind="ExternalOutput").ap()

    with ExitStack() as stack:
        P = Prog(nc, stack)
        P.epsc = make_eps(P)
        X = P.sb([128, KC, nt], "X")
        H = P.sb([128, KC, nt], "H", dtype=FR)
        PK = 16
        WS = [P.sb([128, PK * 128], f"ws{i}") for i in range(2)]
        WF = [P.sb([128, PK * 128], f"wf{i}", dtype=FR) for i in range(2)]
        OST = P.sb([128, 8, nt], "ost")
        G = P.sb([128, 4 * KC], "gains_sb")
        CW = P.sb([128, 2 * NF * 3], "convw_sb")
        ones = P.sb([128, 128], "ones", dtype=FR)
        rstd = P.sb([128, nt], "rstd")
        tmpA = [P.sb([128, nt], f"tmpA{i}") for i in range(2)]
        sqr = [P.sb([128, nt], f"sqr{i}", dtype=FR) for i in range(2)]
        tmpB = [P.sb([128, nt], f"tmpB{i}") for i in range(2)]
        tmpC = [P.sb([128, nt], f"tmpC{i}") for i in range(2)]
        BIG = P.sb([128, NF, nt], "BIG", dtype=FR)
        PS = [P.ps(f"ps{i}") for i in range(8)]
        psi = [0]
        wri = [0]

        def next_ps():
            p = PS[psi[0] % 6]
            psi[0] += 1
            return p

        SSP = [PS[6], PS[7]]

        def wmm(ps, wsrc, nkc, rhs_obj, rhs_fn):
            for p0 in range(0, nkc, PK):
                npc = min(PK, nkc - p0)
                st_, wf = WS[wri[0] % 2], WF[wri[0] % 2]
                wri[0] += 1
                P.dma(st_[:, :npc * 128], wsrc[:, p0 * 128:(p0 + npc) * 128], writes=(st_,))
                P.issue("vector", lambda e, st_=st_, wf=wf, npc=npc: e.tensor_copy(out=wf[:, :npc * 128], in_=st_[:, :npc * 128]),
                        reads=(st_,), writes=(wf,))
                for j in range(npc):
                    mm(P, ps, ps[:, :nt], wf, wf[:, j * 128:(j + 1) * 128], rhs_obj, rhs_fn(p0 + j),
                       p0 + j == 0, p0 + j == nkc - 1)

        ones32 = P.sb([128, 128], "ones32")
        P.issue("vector", lambda e: e.memset(ones32[:], 1.0), writes=(ones32,))
        P.issue("vector", lambda e: e.tensor_copy(out=ones[:], in_=ones32[:]), reads=(ones32,), writes=(ones,))
        zer2 = P.sb([128, 2], "zer2")
        P.issue("vector", lambda e: e.memset(zer2[:], 0.0), writes=(zer2,))
        P.dma(G[:], gains, writes=(G,))
        P.dma(CW[:], convw, writes=(CW,))

        def norm_stats(src, ssp):
            for kc in range(KC):
                sq = sqr[kc % 2]
                act(P, sq, sq[:, :], src, src[:, kc, :], AF.Square)
                mm(P, ssp, ssp[:, :nt], ones, ones[:, :], sq, sq[:, :], kc == 0, kc == KC - 1)

        for s in range(nsh):
            P.dma(X[:], xT[s].rearrange("(kc p) t -> p kc t", p=128), writes=(X,))
            norm_stats(X, SSP[0])
            rstd_from_ss(P, rstd, SSP[0], D, nt)
            for kc in range(KC):
                stt(P, H, H[:, kc, :], X, X[:, kc, :], G[:, kc:kc + 1], rstd, rstd[:, :], ALU.mult, ALU.mult,
                    extra_reads=(G,))
            for n in range(4):
                On_lo = KC + 8 * (n % 2)
                P.dma(OST[:], oT[s, n * 1024:(n + 1) * 1024, :].rearrange("(kc p) t -> p kc t", p=128), writes=(OST,))
                P.issue("gpsimd", lambda e, On_lo=On_lo: e.tensor_copy(out=BIG[:, On_lo:On_lo + 8, :], in_=OST[:]),
                        reads=(OST,), writes=(BIG,))
                for dt in range(KC):
                    gps = next_ps()
                    wmm(gps, wg[n * KC + dt], KC, H, lambda kc: H[:, kc, :])
                    yps = next_ps()
                    wmm(yps, wb[n * KC + dt], 8, BIG, lambda kc, On_lo=On_lo: BIG[:, On_lo + kc, :])
                    sig = tmpB[dt % 2]
                    act(P, sig, sig[:, :], gps, gps[:, :nt], AF.Sigmoid)
                    if n == 0:
                        tt(P, BIG, BIG[:, dt, :], sig, sig[:, :], yps, yps[:, :nt], ALU.mult)
                    else:
                        tm = tmpC[dt % 2]
                        tt(P, tm, tm[:, :], sig, sig[:, :], yps, yps[:, :nt], ALU.mult)
                        tt(P, BIG, BIG[:, dt, :], BIG, BIG[:, dt, :].bitcast(F32), tm, tm[:, :], ALU.add)
            ssp = SSP[1]
            for dt in range(KC):
                zps = next_ps()
                wmm(zps, wo[dt], KC, BIG, lambda kc: BIG[:, kc, :])
                act(P, H, H[:, dt, :], zps, zps[:, :nt], AF.Copy)
                sq = sqr[dt % 2]
                act(P, sq, sq[:, :], zps, zps[:, :nt], AF.Square)
                mm(P, ssp, ssp[:, :nt], ones, ones[:, :], sq, sq[:, :], dt == 0, dt == KC - 1)
            rstd_from_ss(P, rstd, ssp, D, nt)
            for kc in range(KC):
                tm = tmpC[kc % 2]
                stt(P, tm, tm[:, :], H, H[:, kc, :].bitcast(F32), G[:, KC + kc:KC + kc + 1], rstd, rstd[:, :], ALU.mult, ALU.mult,
                    extra_reads=(G,))
                tt(P, X, X[:, kc, :], X, X[:, kc, :], tm, tm[:, :], ALU.add)
            norm_stats(X, SSP[0])
            rstd_from_ss(P, rstd, SSP[0], D, nt)
            for kc in range(KC):
                stt(P, H, H[:, kc, :], X, X[:, kc, :], G[:, 2 * KC + kc:2 * KC + kc + 1], rstd, rstd[:, :],
                    ALU.mult, ALU.mult, extra_reads=(G,))
            for f in range(NF):
                res = []
                for half in range(2):
                    ft = half * NF + f
                    ups = next_ps()
                    wmm(ups, wu[ft], KC, H, lambda kc: H[:, kc, :])
                    u = tmpA[half] if half == 0 else tmpB[0]
                    act(P, u, u[:, :], ups, ups[:, :nt], AF.Copy)
                    c = tmpC[half]
                    ts(P, c, c[:, 2:nt], u, u[:, 2:nt], CW[:, ft * 3 + 2:ft * 3 + 3], ALU.mult, extra_reads=(CW,))
                    stt(P, c, c[:, 2:nt], u, u[:, 1:nt - 1], CW[:, ft * 3 + 1:ft * 3 + 2], c, c[:, 2:nt],
                        ALU.mult, ALU.add, extra_reads=(CW,))
                    stt(P, c, c[:, 2:nt], u, u[:, 0:nt - 2], CW[:, ft * 3:ft * 3 + 1], c, c[:, 2:nt],
                        ALU.mult, ALU.add, extra_reads=(CW,))
                    res.append(c)
                sl = tmpB[1]
                act(P, sl, sl[:, 2:nt], res[0], res[0][:, 2:nt], AF.Silu)
                tt(P, BIG, BIG[:, f, 2:nt], sl, sl[:, 2:nt], res[1], res[1][:, 2:nt], ALU.mult)
                if s == 0:
                    P.issue("gpsimd", lambda e, f=f: e.tensor_copy(out=BIG[:, f, 0:2], in_=zer2[:, 0:2]), reads=(zer2,), writes=(BIG,))
            ssp = SSP[1]
            for dt in range(KC):
                zps = next_ps()
                wmm(zps, wd[dt], NF, BIG, lambda f: BIG[:, f, :])
                act(P, H, H[:, dt, :], zps, zps[:, :nt], AF.Copy)
                sq = sqr[dt % 2]
                act(P, sq, sq[:, :], zps, zps[:, :nt], AF.Square)
                mm(P, ssp, ssp[:, :nt], ones, ones[:, :], sq, sq[:, :], dt == 0, dt == KC - 1)
            rstd_from_ss(P, rstd, ssp, D, nt)
            for kc in range(KC):
                tm = tmpC[kc % 2]
                stt(P, tm, tm[:, :], H, H[:, kc, :].bitcast(F32), G[:, 3 * KC + kc:3 * KC + kc + 1], rstd, rstd[:, :],
                    ALU.mult, ALU.mult, extra_reads=(G,))
                tt(P, X, X[:, kc, :], X, X[:, kc, :], tm, tm[:, :], ALU.add)
            P.dma(yT[s].rearrange("(kc p) t -> p kc t", p=128), X[:], reads=(X,))
        P.emit()
    return nc


def tile_w(w, kc_inner=True):
    K, N = w.shape
    return np.ascontiguousarray(w.reshape(K // 128, 128, N // 128, 128).transpose(2, 1, 0, 3)).reshape(N // 128, 128, K)


def dense_weights(i, w_branch, w_gate, w_out, w_up, w_down, ffn_conv, pre_mix, post_mix, pre_ffn, post_ffn):
    KC = D // 128
    wgt = tile_w(w_gate[i])
    wbt = np.concatenate([tile_w(w_branch[i, n]) for n in range(4)], axis=0)
    wot = tile_w(w_out[i])
    wut = tile_w(w_up[i])
    wdt = tile_w(w_down[i])
    g = np.stack([pre_mix[i], post_mix[i], pre_ffn[i], post_ffn[i]], 0).reshape(4, KC, 128).transpose(2, 0, 1)
    g = np.ascontiguousarray(g).reshape(128, 4 * KC)
    cw = np.ascontiguousarray(ffn_conv[i].reshape(3, 2 * DFF // 128, 128).transpose(2, 1, 0)).reshape(128, -1)
    return dict(wg=wgt, wb=wbt, wo=wot, wu=wut, wd=wdt, gains=g.astype(np.float32), convw=cw.astype(np.float32))


def run_dense(xT_full, oT_full, wts, nsh_total=16):
    B, _, L = xT_full.shape
    ns = L // nsh_total
    nt = ns + 2
    nt += nt & 1
    per_core = B * nsh_total // NCORES
    xp = np.concatenate([np.zeros((B, D, 2), np.float32), xT_full, np.zeros((B, D, 2), np.float32)], axis=2)
    op = np.concatenate([np.zeros((B, D, 2), np.float32), oT_full, np.zeros((B, D, 2), np.float32)], axis=2)
    in_maps = []
    for c in range(NCORES):
        xs, os_ = [], []
        for j in range(per_core):
            g = c * per_core + j
            b, k = divmod(g, nsh_total)
            xs.append(xp[b, :, k * ns:k * ns + nt])
            os_.append(op[b, :, k * ns:k * ns + nt])
        m = dict(wts)
        m["xT"] = np.ascontiguousarray(np.stack(xs, 0))
        m["oT"] = np.ascontiguousarray(np.stack(os_, 0))
        in_maps.append(m)
    nc = build_dense(per_core, nt)
    res = run_bass_kernel_spmd(nc, in_maps, core_ids=list(range(NCORES)), **({"trace": True} if TRACE else {}))
    LAST_NS.append(res.exec_time_ns)
    out = np.zeros((B, D, L), np.float32)
    for c in range(NCORES):
        y = res.results[c]["yT"]
        for j in range(per_core):
            g = c * per_core + j
            b, k = divmod(g, nsh_total)
            out[b, :, k * ns:(k + 1) * ns] = y[j][:, 2:2 + ns]
    return out


ENG_A = "vector"
NT_IN = 34
PADL = 4
RQ, RK, RV, RG = 0, 2, 4, 6
AQ, AK, AV, AG, ALR = 8, 9, 10, 12, 14
CQ, CK, CV, CZ, CAB = 15, 17, 19, 21, 23
DR, DK, DV, DW, DA, DG = 24, 26, 28, 30, 31, 32


def build_mixer(T):
    nc = bass.Bass("TRN2", target_bir_lowering=False)
    KC = D // 128
    TB = 512
    blocks = [(t0, min(TB, T - t0)) for t0 in range(0, T, TB)]
    xT = nc.dram_tensor("xT", [D, T], F32, kind="ExternalInput").ap()
    win = nc.dram_tensor("win", [NT_IN, 128, KC * 128], F32, kind="ExternalInput").ap()
    gpre = nc.dram_tensor("gpre", [128, KC], F32, kind="ExternalInput").ap()
    rope = nc.dram_tensor("rope", [2, 128, T], F32, kind="ExternalInput").ap()
    pv = nc.dram_tensor("pv", [128, 64], F32, kind="ExternalInput").ap()
    gw2 = nc.dram_tensor("gw2", [16, 128], F32, kind="ExternalInput").ap()
    rw2 = nc.dram_tensor("rw2", [64, 256], F32, kind="ExternalInput").ap()
    ra2 = nc.dram_tensor("ra2", [64, 256], F32, kind="ExternalInput").ap()
    rg2 = nc.dram_tensor("rg2", [256, 256], F32, kind="ExternalInput").ap()
    consts = nc.dram_tensor("consts", [3, 128, 128], F32, kind="ExternalInput").ap()
    outT = nc.dram_tensor("outT", [1024, T], F32, kind="ExternalOutput").ap()
    PV_GAM, PV_RETN, PV_GLAB, PV_GLAN, PV_CONV, PV_ALOG, PV_DTB, PV_GDNN = 0, 1, 3, 4, 6, 30, 31, 32
    PV_MU, PV_W0, PV_A0, PV_KK, PV_KA, PV_RK, PV_LNW, PV_LNB = 33, 43, 45, 47, 49, 51, 53, 55

    with ExitStack() as stack:
        P = Prog(nc, stack)
        P.epsc = make_eps(P)
        PR = P.dram([NT_IN * 128, PADL + T], "PR")
        BRr = P.dram([T, 1, 512], "BRr")
        BRa = P.dram([T, 1, 384], "BRa")
        BRc = P.dram([T, 1, 1280], "BRc")
        BRd = P.dram([T, 2, 640], "BRd")
        VTc = P.dram([128, 2, T], "VTc")
        VTd = P.dram([128, 2, T], "VTd")
        OS = [P.dram([128, 2, T], f"OS{i}") for i in range(4)]
        BON = P.dram([256, T], "BON")
        GT = P.dram([256, T], "GT")
        OUT = Obj(outT, "outT")

        PVs = P.sb([128, 64], "pv_sb")
        CN = P.sb([128, 3, 128], "consts_sb")
        Gp = P.sb([128, KC], "gpre_sb")
        PS = [P.ps(f"ps{i}") for i in range(8)]
        psi = [0]

        def next_ps():
            p = PS[psi[0] % 7]
            psi[0] += 1
            return p
        SSP = PS[7]
        P.dma(PVs[:], pv, writes=(PVs,))
        P.dma(CN[:], consts.rearrange("a p c -> p a c"), writes=(CN,))
        P.dma(Gp[:], gpre, writes=(Gp,))
        ident, ones, bones = CN[:, 0, :], CN[:, 1, :], CN[:, 2, :]
        act(P, PVs, PVs[:, 30:31], PVs, PVs[:, 30:31], AF.Exp)
        ts(P, PVs, PVs[:, 30:31], PVs, PVs[:, 30:31], -1.0, ALU.mult)

        def pcol(c):
            return PVs[:, c:c + 1]

        with ExitStack() as st1:
            P.stack = st1
            X = P.sb([128, KC, TB], "X")
            H = P.sb([128, KC, TB], "H", dtype=FR)
            PK = 16
            WS = [P.sb([128, PK * 128], f"ws{i}") for i in range(2)]
            WF = [P.sb([128, PK * 128], f"wf{i}", dtype=FR) for i in range(2)]
            sqt = [P.sb([128, TB], f"sq{i}") for i in range(2)]
            ev = [P.sb([128, TB], f"ev{i}") for i in range(2)]
            rstd = P.sb([128, TB], "rstd")
            zt = P.sb([128, PADL], "zt")
            P.issue("vector", lambda e: e.memset(zt[:], 0.0), writes=(zt,))
            for ct in range(NT_IN):
                P.dma(PR[ct * 128:(ct + 1) * 128, 0:PADL], zt[:], reads=(zt,), writes=(PR,))
            wi = 0
            for (t0, tb) in blocks:
                P.dma(X[:, :, :tb], xT[:, t0:t0 + tb].rearrange("(kc p) t -> p kc t", p=128), writes=(X,))
                for kc in range(KC):
                    sq = sqt[kc % 2]
                    act(P, sq, sq[:, :tb], X, X[:, kc, :tb], AF.Square)
                    mm(P, SSP, SSP[:, :tb], CN, ones, sq, sq[:, :tb], kc == 0, kc == KC - 1)
                rstd_from_ss(P, rstd, SSP, D, tb)
                for kc in range(KC):
                    stt(P, H, H[:, kc, :tb], X, X[:, kc, :tb], Gp[:, kc:kc + 1], rstd, rstd[:, :tb],
                        ALU.mult, ALU.mult, extra_reads=(Gp,))
                for ct in range(NT_IN):
                    ps = next_ps()
                    for p0 in range(0, KC, PK):
                        st_, wf = WS[wi % 2], WF[wi % 2]
                        wi += 1
                        P.dma(st_[:], win[ct][:, p0 * 128:(p0 + PK) * 128], writes=(st_,))
                        P.issue("vector", lambda e, st_=st_, wf=wf: e.tensor_copy(out=wf[:], in_=st_[:]),
                                reads=(st_,), writes=(wf,))
                        for j in range(PK):
                            mm(P, ps, ps[:, :tb], wf, wf[:, j * 128:(j + 1) * 128], H, H[:, p0 + j, :tb],
                               p0 + j == 0, p0 + j == KC - 1)
                    e_ = ev[ct % 2]
                    act(P, e_, e_[:, :tb], ps, ps[:, :tb], AF.Copy)
                    P.dma(PR[ct * 128:(ct + 1) * 128, PADL + t0:PADL + t0 + tb], e_[:, :tb], reads=(e_,), writes=(PR,))
        P.stack = stack
        P.barrier()

        with ExitStack() as st2:
            P.stack = st2
            NTMP = 14
            tp = [P.sb([128, TB + PADL], f"tp{i}") for i in range(NTMP)]
            rows = [P.sb([128, 128], f"rows{i}") for i in range(4)]
            P_rows5 = P.sb([128, 5, 128], "rows5")
            ri = [0]

            def load(dst, tile_idx, t0, tb, halo=0, nrows=128):
                P.dma(dst[:nrows, :tb + halo],
                      PR[tile_idx * 128:tile_idx * 128 + nrows, PADL + t0 - halo:PADL + t0 + tb], reads=(PR,), writes=(dst,))

            def to_rows(src, tb, dst_obj, dst_fn):
                for s0 in range(0, tb, 128):
                    n = min(128, tb - s0)
                    ps = next_ps()
                    P.issue("tensor", lambda e, ps=ps, s0=s0, n=n: e.transpose(ps[:n, :128], src[:, s0:s0 + n], ident),
                            reads=(src, CN), writes=(ps,))
                    r = rows[ri[0] % 4]
                    ri[0] += 1
                    act(P, r, r[:n, :], ps, ps[:n, :128], AF.Copy)
                    yield r, s0, n

            def rows_out(src, tb, dst, t0, ph, off, width=128, c0=0):
                for r, s0, n in to_rows(src, tb, dst, None):
                    P.dma(dst[t0 + s0:t0 + s0 + n, ph, off:off + width], r[:n, c0:c0 + width], reads=(r,), writes=(dst,))

            def group_sum(dst, srcs, tb, which):
                for i, s_ in enumerate(srcs):
                    mm(P, dst, dst[:, :tb], CN, which, s_, s_[:, :tb], i == 0, i == len(srcs) - 1)

            for (t0, tb) in blocks:
                q1, q2, k1, k2, cs, sn, a_, b_, c_, d_ = tp[:10]
                load(q1, RQ, t0, tb); load(q2, RQ + 1, t0, tb); load(k1, RK, t0, tb); load(k2, RK + 1, t0, tb)
                P.dma(cs[:, :tb], rope[0, :, t0:t0 + tb], writes=(cs,))
                P.dma(sn[:, :tb], rope[1, :, t0:t0 + tb], writes=(sn,))
                for (x1, x2, off, scl) in ((k1, k2, 0, 1.0), (q1, q2, 256, 256 ** -0.5)):
                    tt(P, a_, a_[:, :tb], x1, x1[:, :tb], cs, cs[:, :tb], ALU.mult)
                    tt(P, b_, b_[:, :tb], x2, x2[:, :tb], sn, sn[:, :tb], ALU.mult)
                    tt(P, a_, a_[:, :tb], a_, a_[:, :tb], b_, b_[:, :tb], ALU.subtract)
                    tt(P, c_, c_[:, :tb], x1, x1[:, :tb], sn, sn[:, :tb], ALU.mult)
                    tt(P, d_, d_[:, :tb], x2, x2[:, :tb], cs, cs[:, :tb], ALU.mult)
                    tt(P, c_, c_[:, :tb], c_, c_[:, :tb], d_, d_[:, :tb], ALU.add)
                    if scl != 1.0:
                        ts(P, a_, a_[:, :tb], a_, a_[:, :tb], scl, ALU.mult)
                        ts(P, c_, c_[:, :tb], c_, c_[:, :tb], scl, ALU.mult)
                    rows_out(a_, tb, BRr, t0, 0, off)
                    rows_out(c_, tb, BRr, t0, 0, off + 128)
                lr, al, kk_, qq_ = tp[:4]
                load(lr, ALR, t0, tb, nrows=16)
                g2t = tp[4]
                P.dma(g2t[:16, :128], gw2, writes=(g2t,))
                ps = next_ps()
                mm(P, ps, ps[:, :tb], g2t, g2t[:16, :128], lr, lr[:16, :tb], True, True)
                nb_ = tp[5]
                ts(P, nb_, nb_[:, 0:1], PVs, pcol(PV_GLAB), -1.0, ALU.mult)
                act(P, al, al[:, :tb], ps, ps[:, :tb], AF.Exp, scale=-1.0, bias=nb_[:, 0:1], extra_reads=(nb_,))
                act(P, al, al[:, :tb], al, al[:, :tb], AF.Ln, scale=1.0, bias=P.epsc[:, 2:3], extra_reads=(P.epsc,))
                act(P, al, al[:, :tb], al, al[:, :tb], AF.Exp, scale=-1.0 / 16.0)
                rows_out(al, tb, BRa, t0, 0, 0)
                load(kk_, AK, t0, tb)
                rows_out(kk_, tb, BRa, t0, 0, 128)
                load(qq_, AQ, t0, tb)
                ts(P, qq_, qq_[:, :tb], qq_, qq_[:, :tb], 128 ** -0.5, ALU.mult)
                rows_out(qq_, tb, BRa, t0, 0, 256)
                ab, eg = tp[0], tp[1]
                load(ab, CAB, t0, tb)
                act(P, eg, eg[:, :tb], ab, ab[:, :tb], AF.Exp, scale=1.0, bias=pcol(PV_DTB), extra_reads=(PVs,))
                act(P, eg, eg[:, :tb], eg, eg[:, :tb], AF.Ln, scale=1.0, bias=P.epsc[:, 2:3], extra_reads=(P.epsc,))
                ts(P, eg, eg[:, :tb], eg, eg[:, :tb], pcol(PV_ALOG), ALU.mult, extra_reads=(PVs,))
                act(P, eg, eg[:, :tb], eg, eg[:, :tb], AF.Exp)
                bt = tp[2]
                act(P, bt, bt[:, :tb], ab, ab[:, :tb], AF.Sigmoid)
                egr = [None] * 8
                cv = {}
                for nm, base, pvoff in (("q", CQ, 0), ("k", CK, 2), ("v", CV, 4)):
                    for hh in range(2):
                        src = tp[3]
                        load(src, base + hh, t0, tb, halo=3)
                        dst = tp[4 + len(cv)]
                        wc = PV_CONV + (pvoff + hh) * 4
                        ts(P, dst, dst[:, :tb], src, src[:, 3:3 + tb], pcol(wc + 3), ALU.mult, extra_reads=(PVs,))
                        for j in range(3):
                            stt(P, dst, dst[:, :tb], src, src[:, j:j + tb], pcol(wc + j), dst, dst[:, :tb],
                                ALU.mult, ALU.add, extra_reads=(PVs,))
                        act(P, dst, dst[:, :tb], dst, dst[:, :tb], AF.Silu)
                        cv[(nm, hh)] = dst
                sqq = tp[10]
                rn = tp[11]
                for nm, scl in (("q", 128 ** -0.5), ("k", 1.0)):
                    for hh in range(2):
                        x_ = cv[(nm, hh)]
                        tt(P, sqq, sqq[:, :tb], x_, x_[:, :tb], x_, x_[:, :tb], ALU.mult)
                        ps = next_ps()
                        group_sum(ps, [sqq], tb, ones)
                        rsqrt(P, rn, rn[:, :tb], ps, ps[:, :tb], 1.0, P.epsc[:, 0:1], P.epsc)
                        if scl != 1.0:
                            ts(P, rn, rn[:, :tb], rn, rn[:, :tb], scl, ALU.mult)
                        tt(P, x_, x_[:, :tb], x_, x_[:, :tb], rn, rn[:, :tb], ALU.mult)
                for hh in range(2):
                    v_ = cv[("v", hh)]
                    P.dma(VTc[:, hh, t0:t0 + tb], v_[:, :tb], reads=(v_,), writes=(VTc,))
                for s0 in range(0, tb, 128):
                    n = min(128, tb - s0)
                    pe = next_ps()
                    P.issue("tensor", lambda e, pe=pe, s0=s0, n=n: e.transpose(pe[:n, :128], eg[:, s0:s0 + n], ident),
                            reads=(eg, CN), writes=(pe,))
                    pb = next_ps()
                    P.issue("tensor", lambda e, pb=pb, s0=s0, n=n: e.transpose(pb[:n, :128], bt[:, s0:s0 + n], ident),
                            reads=(bt, CN), writes=(pb,))
                    sc = rows[ri[0] % 4]; ri[0] += 1
                    act(P, sc, sc[:n, 0:4], pe, pe[:n, 0:4], AF.Copy)
                    P.issue("scalar", lambda e, sc=sc, pb=pb, n=n: e.activation(out=sc[:n, 2:4], in_=pb[:n, 2:4], func=AF.Copy),
                            reads=(pb,), writes=(sc,))
                    for hh in range(2):
                        pk = next_ps()
                        kx = cv[("k", hh)]
                        P.issue("tensor", lambda e, pk=pk, kx=kx, s0=s0, n=n: e.transpose(pk[:n, :128], kx[:, s0:s0 + n], ident),
                                reads=(kx, CN), writes=(pk,))
                        r5 = P_rows5
                        ts(P, r5, r5[:n, 0, :], CN, ones[:n, :], sc[:n, hh:hh + 1], ALU.mult, extra_reads=(sc,))
                        act(P, r5, r5[:n, 1, :], pk, pk[:n, :128], AF.Copy)
                        ts(P, r5, r5[:n, 3, :], pk, pk[:n, :128], sc[:n, 2 + hh:3 + hh], ALU.mult, extra_reads=(sc,))
                        ts(P, r5, r5[:n, 2, :], r5, r5[:n, 3, :], sc[:n, hh:hh + 1], ALU.mult, extra_reads=(sc,))
                        ts(P, r5, r5[:n, 2, :], r5, r5[:n, 2, :], -1.0, ALU.mult)
                        pq = next_ps()
                        qx = cv[("q", hh)]
                        P.issue("tensor", lambda e, pq=pq, qx=qx, s0=s0, n=n: e.transpose(pq[:n, :128], qx[:, s0:s0 + n], ident),
                                reads=(qx, CN), writes=(pq,))
                        act(P, r5, r5[:n, 4, :], pq, pq[:n, :128], AF.Copy)
                        P.dma(BRc[t0 + s0:t0 + s0 + n, 0, :].rearrange("t (v g k) -> t v g k", v=5, g=2)[:, :, hh, :],
                              r5[:n, :, :], reads=(r5,), writes=(BRc,))
                sh = {}
                for nm, base, ntile, mu0 in (("r", DR, 2, 0), ("k", DK, 2, 2), ("v", DV, 2, 4), ("w", DW, 1, 6),
                                             ("a", DA, 1, 7), ("g", DG, 2, 8)):
                    for i in range(ntile):
                        src = tp[13]
                        load(src, base + i, t0, tb, halo=1)
                        dst = tp[len(sh)]
                        tt(P, dst, dst[:, :tb], src, src[:, 0:tb], src, src[:, 1:1 + tb], ALU.subtract)
                        stt(P, dst, dst[:, :tb], dst, dst[:, :tb], pcol(PV_MU + mu0 + i), src, src[:, 1:1 + tb],
                            ALU.mult, ALU.add, extra_reads=(PVs,))
                        sh[(nm, i)] = dst
                wt_ = tp[10]
                lowr = tp[11]
                act(P, sh[("w", 0)], sh[("w", 0)][:, :tb], sh[("w", 0)], sh[("w", 0)][:, :tb], AF.Tanh)
                for i in range(2):
                    act(P, sh[("g", i)], sh[("g", i)][:, :tb], sh[("g", i)], sh[("g", i)][:, :tb], AF.Sigmoid)
                for i in range(2):
                    r_, k_, v_ = sh[("r", i)], sh[("k", i)], sh[("v", i)]
                    P.dma(VTd[:, i, t0:t0 + tb], v_[:, :tb], reads=(v_,), writes=(VTd,))
                    P.dma(lowr[:64, :128], rw2[:, i * 128:(i + 1) * 128], writes=(lowr,))
                    ps = next_ps()
                    mm(P, ps, ps[:, :tb], lowr, lowr[:64, :128], sh[("w", 0)], sh[("w", 0)][:64, :tb], True, True)
                    dec = tp[12]
                    act(P, dec, dec[:, :tb], ps, ps[:, :tb], AF.Sigmoid, bias=pcol(PV_W0 + i), extra_reads=(PVs,))
                    act(P, dec, dec[:, :tb], dec, dec[:, :tb], AF.Exp, scale=-0.606531)
                    for hh in range(2):
                        rows_out(dec, tb, BRd, t0, hh, (0 * 2 + i) * 64, width=64, c0=hh * 64)
                    P.dma(lowr[:64, :128], ra2[:, i * 128:(i + 1) * 128], writes=(lowr,))
                    ps = next_ps()
                    mm(P, ps, ps[:, :tb], lowr, lowr[:64, :128], sh[("a", 0)], sh[("a", 0)][:64, :tb], True, True)
                    aa = tp[12]
                    act(P, aa, aa[:, :tb], ps, ps[:, :tb], AF.Sigmoid, bias=pcol(PV_A0 + i), extra_reads=(PVs,))
                    ts(P, wt_, wt_[:, :tb], k_, k_[:, :tb], pcol(PV_KK + i), ALU.mult, extra_reads=(PVs,))
                    sq_ = tp[13]
                    tt(P, sq_, sq_[:, :tb], wt_, wt_[:, :tb], wt_, wt_[:, :tb], ALU.mult)
                    ps = next_ps()
                    group_sum(ps, [sq_], tb, bones)
                    rsqrt(P, sq_, sq_[:, :tb], ps, ps[:, :tb], 1.0, P.epsc[:, 0:1], P.epsc)
                    tt(P, wt_, wt_[:, :tb], wt_, wt_[:, :tb], sq_, sq_[:, :tb], ALU.mult)
                    for hh in range(2):
                        rows_out(wt_, tb, BRd, t0, hh, (1 * 2 + i) * 64, width=64, c0=hh * 64)
                    tt(P, wt_, wt_[:, :tb], wt_, wt_[:, :tb], aa, aa[:, :tb], ALU.mult)
                    ts(P, wt_, wt_[:, :tb], wt_, wt_[:, :tb], -1.0, ALU.mult)
                    for hh in range(2):
                        rows_out(wt_, tb, BRd, t0, hh, (2 * 2 + i) * 64, width=64, c0=hh * 64)
                    ts(P, aa, aa[:, :tb], aa, aa[:, :tb], -1.0, ALU.add)
                    ts(P, aa, aa[:, :tb], aa, aa[:, :tb], pcol(PV_KA + i), ALU.mult, extra_reads=(PVs,))
                    ts(P, aa, aa[:, :tb], aa, aa[:, :tb], 1.0, ALU.add)
                    tt(P, k_, k_[:, :tb], k_, k_[:, :tb], aa, aa[:, :tb], ALU.mult)
                    for hh in range(2):
                        rows_out(k_, tb, BRd, t0, hh, (3 * 2 + i) * 64, width=64, c0=hh * 64)
                        rows_out(r_, tb, BRd, t0, hh, (4 * 2 + i) * 64, width=64, c0=hh * 64)
                    tt(P, sq_, sq_[:, :tb], r_, r_[:, :tb], k_, k_[:, :tb], ALU.mult)
                    ts(P, sq_, sq_[:, :tb], sq_, sq_[:, :tb], pcol(PV_RK + i), ALU.mult, extra_reads=(PVs,))
                    ps = next_ps()
                    group_sum(ps, [sq_], tb, bones)
                    tt(P, sq_, sq_[:, :tb], ps, ps[:, :tb], v_, v_[:, :tb], ALU.mult)
                    P.dma(BON[i * 128:(i + 1) * 128, t0:t0 + tb], sq_[:, :tb], reads=(sq_,), writes=(BON,))
                    ps = next_ps()
                    P.dma(lowr[:, :128], rg2[0:128, i * 128:(i + 1) * 128], writes=(lowr,))
                    mm(P, ps, ps[:, :tb], lowr, lowr[:, :128], sh[("g", 0)], sh[("g", 0)][:, :tb], True, False)
                    lowr2 = tp[12]
                    P.dma(lowr2[:, :128], rg2[128:256, i * 128:(i + 1) * 128], writes=(lowr2,))
                    mm(P, ps, ps[:, :tb], lowr2, lowr2[:, :128], sh[("g", 1)], sh[("g", 1)][:, :tb], False, True)
                    act(P, sq_, sq_[:, :tb], ps, ps[:, :tb], AF.Copy)
                    P.dma(GT[i * 128:(i + 1) * 128, t0:t0 + tb], sq_[:, :tb], reads=(sq_,), writes=(GT,))
        P.stack = stack

        P.barrier()
        st3 = ExitStack()
        P.stack = st3

        def scan(eng, dq, TC, BR, PH, Wd, vsrc_fn, OD, G, K, vecs, lowrank, shared, gam=None):
            S = P.sb([128, G, K], P.uid("S"))
            tmp = P.sb([128, G, K], P.uid("stmp"))
            sa = P.sb([128, G], P.uid("sa"))
            RB = [P.sb([128, TC, Wd], P.uid("rb")) for _ in range(2)]
            VB = [P.sb([128, G, TC], P.uid("vb")) for _ in range(2)]
            OB = [P.sb([128, G, TC], P.uid("ob")) for _ in range(2)]
            TQ = [P.sb([128, G, K], P.uid("tq")) for _ in range(4)]
            junk = P.sb([128, K], P.uid("junk"))
            P.issue(eng, lambda e: e.memset(S[:], 0.0), writes=(S,))
            for ci, c0 in enumerate(range(0, T, TC)):
                n = min(TC, T - c0)
                rb, vb, ob = RB[ci % 2], VB[ci % 2], OB[ci % 2]
                for ph in range(PH):
                    np_ = 128 // PH
                    P.dma(rb[ph * np_:(ph + 1) * np_, :n, :],
                          BR[c0:c0 + n, ph, :].partition_broadcast(np_), reads=(BR,), writes=(rb,), queue=dq)
                vsrc_fn(vb, c0, n, dq)
                for i in range(n):
                    def row(nm):
                        v = vecs[nm]
                        if shared:
                            return rb[:, i, v * K:(v + 1) * K].unsqueeze(1).broadcast_to([128, G, K])
                        return rb[:, i, v * G * K:(v + 1) * G * K].rearrange("p (g k) -> p g k", g=G)

                    def rowg(nm, g):
                        v = vecs[nm]
                        if shared:
                            return rb[:, i, v * K:(v + 1) * K]
                        return rb[:, i, (v * G + g) * K:(v * G + g + 1) * K]
                    if lowrank:
                        tt(P, tmp, tmp[:], S, S[:], rb, row("kk"), ALU.mult, eng=eng)
                        yield
                        P.issue(eng, lambda e: e.tensor_reduce(out=sa[:], in_=tmp[:], axis=AX.X, op=ALU.add),
                                reads=(tmp,), writes=(sa,))
                        yield
                    if gam is not None:
                        ts(P, S, S[:], S, S[:], gam, ALU.mult, extra_reads=(PVs,), eng=eng)
                    else:
                        tt(P, S, S[:], S, S[:], rb, row("w"), ALU.mult, eng=eng)
                    yield
                    if lowrank:
                        for g in range(G):
                            stt(P, S, S[:, g, :], rb, rowg("nb", g), sa[:, g:g + 1], S, S[:, g, :], ALU.mult, ALU.add,
                                extra_reads=(sa,), eng=eng)
                            yield
                    for g in range(G):
                        stt(P, S, S[:, g, :], rb, rowg("kp", g), vb[:, g, i:i + 1], S, S[:, g, :], ALU.mult, ALU.add,
                            extra_reads=(vb,), eng=eng)
                        yield
                    tq = TQ[(ci * TC + i) % 4]
                    tt(P, tq, tq[:], S, S[:], rb, row("q"), ALU.mult, eng=eng)
                    yield
                    for g in range(G):
                        P.issue("scalar", lambda e, ob=ob, i=i, g=g, tq=tq: e.activation(
                            out=junk[:, :], in_=tq[:, g, :], func=AF.Copy, accum_out=ob[:, g, i:i + 1]),
                            reads=(tq,), writes=(ob, junk))
                P.dma(OD[:, :, c0:c0 + n], ob[:, :, :n], reads=(ob,), writes=(OD,), queue=dq)
                yield

        def v_from_pr(tile0):
            def f(vb, c0, n, dq):
                for g in range(2):
                    P.dma(vb[:, g, :n], PR[(tile0 + g) * 128:(tile0 + g + 1) * 128, PADL + c0:PADL + c0 + n],
                          reads=(PR,), writes=(vb,), queue=dq)
            return f

        def v_from_vtc(vb, c0, n, dq):
            P.dma(vb[:, :, :n], VTc[:, :, c0:c0 + n], reads=(VTc,), writes=(vb,), queue=dq)

        def v_from_vtd(vb, c0, n, dq):
            P.dma(vb[:, :, :n], VTd[:, :, c0:c0 + n], reads=(VTd,), writes=(vb,), queue=dq)

        DQA = "scalar" if ENG_A != "vector" else "sync"
        gens = [
            (scan(ENG_A, DQA, 8, BRr, 1, 512, v_from_pr(RV), OS[0], 2, 256, {"kp": 0, "q": 1}, False, True, gam=pcol(PV_GAM)), 1),
            (scan("vector", "sync", 4, BRc, 1, 1280, v_from_vtc, OS[2], 2, 128, {"w": 0, "kk": 1, "nb": 2, "kp": 3, "q": 4}, True, False), 2),
            (scan("vector", "sync", 8, BRa, 1, 384, v_from_pr(AV), OS[1], 2, 128, {"w": 0, "kp": 1, "q": 2}, False, True), 1),
            (scan("vector", "sync", 8, BRd, 2, 640, v_from_vtd, OS[3], 2, 64, {"w": 0, "kk": 1, "nb": 2, "kp": 3, "q": 4}, True, False), 1),
        ]
        alive = [not SKIP_SCANS] * len(gens)
        while any(alive):
            for gi, (gen, reps) in enumerate(gens):
                if alive[gi]:
                    try:
                        next(gen)
                    except StopIteration:
                        alive[gi] = False
        P.barrier()
        st3.close()
        P.stack = stack

        with ExitStack() as st4:
            P.stack = st4
            tq = [P.sb([128, TB], f"tq{i}") for i in range(10)]
            for (t0, tb) in blocks:
                for m in range(4):
                    o0, o1, g0, g1, sq0, sq1, mean, rn = tq[:8]
                    P.dma(o0[:, :tb], OS[m][:, 0, t0:t0 + tb], reads=(OS[m],), writes=(o0,))
                    P.dma(o1[:, :tb], OS[m][:, 1, t0:t0 + tb], reads=(OS[m],), writes=(o1,))
                    oo = [o0, o1]
                    if m in (0, 1):
                        groups, which, n_el = [[0, 1]], ones, 256.0
                    elif m == 2:
                        groups, which, n_el = [[0], [1]], ones, 128.0
                    else:
                        groups, which, n_el = [[0], [1]], bones, 64.0
                    center = m in (0, 3)
                    eps = 64e-5 if m == 3 else EPS
                    for grp in groups:
                        if center:
                            ps = next_ps()
                            group_sum(ps, [oo[g] for g in grp], tb, which)
                            ts(P, mean, mean[:, :tb], ps, ps[:, :tb], 1.0 / n_el, ALU.mult)
                            for g in grp:
                                tt(P, oo[g], oo[g][:, :tb], oo[g], oo[g][:, :tb], mean, mean[:, :tb], ALU.subtract)
                        sqs = [sq0, sq1]
                        for g in grp:
                            tt(P, sqs[g], sqs[g][:, :tb], oo[g], oo[g][:, :tb], oo[g], oo[g][:, :tb], ALU.mult)
                        ps = next_ps()
                        group_sum(ps, [sqs[g] for g in grp], tb, which)
                        rsqrt(P, rn, rn[:, :tb], ps, ps[:, :tb], 1.0 / n_el, P.epsc[:, 1:2] if m == 3 else P.epsc[:, 0:1], P.epsc)
                        for g in grp:
                            ncol = {0: PV_RETN + g, 1: PV_GLAN + g, 2: PV_GDNN, 3: PV_LNW + g}[m]
                            stt(P, oo[g], oo[g][:, :tb], oo[g], oo[g][:, :tb], pcol(ncol), rn, rn[:, :tb],
                                ALU.mult, ALU.mult, extra_reads=(PVs,))
                    for g in range(2):
                        gt_ = [g0, g1][g]
                        if m == 3:
                            ts(P, oo[g], oo[g][:, :tb], oo[g], oo[g][:, :tb], pcol(PV_LNB + g), ALU.add, extra_reads=(PVs,))
                            P.dma(gt_[:, :tb], BON[g * 128:(g + 1) * 128, t0:t0 + tb], reads=(BON,), writes=(gt_,))
                            tt(P, oo[g], oo[g][:, :tb], oo[g], oo[g][:, :tb], gt_, gt_[:, :tb], ALU.add)
                            gt2 = tq[8]
                            P.dma(gt2[:, :tb], GT[g * 128:(g + 1) * 128, t0:t0 + tb], reads=(GT,), writes=(gt2,))
                            tt(P, oo[g], oo[g][:, :tb], oo[g], oo[g][:, :tb], gt2, gt2[:, :tb], ALU.mult)
                        else:
                            gtile = {0: RG, 1: AG, 2: CZ}[m] + g
                            P.dma(gt_[:, :tb], PR[gtile * 128:(gtile + 1) * 128, PADL + t0:PADL + t0 + tb],
                                  reads=(PR,), writes=(gt_,))
                            act(P, gt_, gt_[:, :tb], gt_, gt_[:, :tb], AF.Silu)
                            tt(P, oo[g], oo[g][:, :tb], oo[g], oo[g][:, :tb], gt_, gt_[:, :tb], ALU.mult)
                        P.dma(outT[m * 256 + g * 128:m * 256 + (g + 1) * 128, t0:t0 + tb], oo[g][:, :tb],
                              reads=(oo[g],), writes=(OUT,))
        P.stack = stack
        P.emit()
    return nc


def mixer_cols(jq):
    A_q, A_k, A_v, A_g = 0, 1024, 2048, 3072
    B_q, B_k, B_v, B_g, B_lr = 4096, 4608, 5120, 6144, 7168
    C_q, C_k, C_v, C_z, C_a, C_b = 7184, 8208, 9232, 10256, 11280, 11288
    Ds = 11296
    groups = [
        (np.arange(A_q + jq * 256, A_q + jq * 256 + 256), 256), (np.arange(A_k + jq * 256, A_k + jq * 256 + 256), 256),
        (np.arange(A_v + jq * 256, A_v + jq * 256 + 256), 256), (np.arange(A_g + jq * 256, A_g + jq * 256 + 256), 256),
        (np.arange(B_q + jq * 128, B_q + jq * 128 + 128), 128), (np.arange(B_k + jq * 128, B_k + jq * 128 + 128), 128),
        (np.arange(B_v + jq * 256, B_v + jq * 256 + 256), 256), (np.arange(B_g + jq * 256, B_g + jq * 256 + 256), 256),
        (np.arange(B_lr, B_lr + 16), 128),
        (np.arange(C_q + jq * 256, C_q + jq * 256 + 256), 256), (np.arange(C_k + jq * 256, C_k + jq * 256 + 256), 256),
        (np.arange(C_v + jq * 256, C_v + jq * 256 + 256), 256), (np.arange(C_z + jq * 256, C_z + jq * 256 + 256), 256),
        (np.array([C_a + 2 * jq, C_a + 2 * jq + 1, C_b + 2 * jq, C_b + 2 * jq + 1]), 128),
        (np.arange(Ds + jq * 256, Ds + jq * 256 + 256), 256), (np.arange(Ds + 1024 + jq * 256, Ds + 1024 + jq * 256 + 256), 256),
        (np.arange(Ds + 2048 + jq * 256, Ds + 2048 + jq * 256 + 256), 256),
        (np.arange(Ds + 3072, Ds + 3136), 128), (np.arange(Ds + 3136, Ds + 3200), 128), (np.arange(Ds + 3200, Ds + 3360), 256),
    ]
    return groups


def mixer_params(i, jq, T, p):
    groups = mixer_cols(jq)
    w_in = p["w_in"][i]
    W = np.zeros((D, NT_IN * 128), np.float32)
    off = 0
    for cols, width in groups:
        W[:, off:off + len(cols)] = w_in[:, cols]
        off += width
    assert off == NT_IN * 128
    pv = np.zeros((128, 64), np.float32)
    pp = np.arange(128)
    pv[:, 0] = 1.0 - 2.0 ** (-5.0 - jq)
    for g in range(2):
        pv[:, 1 + g] = p["ret_norm"][i][jq * 256 + g * 128 + pp]
        pv[:, 4 + g] = p["gla_norm"][i][jq * 256 + g * 128 + pp]
    pv[:, 3] = p["gla_b"][i][jq * 128 + pp]
    for pvoff, base in ((0, 0), (2, 1024), (4, 2048)):
        for hh in range(2):
            for j in range(4):
                pv[:, 6 + (pvoff + hh) * 4 + j] = p["gdn_conv"][i][j, base + (2 * jq + hh) * 128 + pp]
    pv[0:2, 30] = p["gdn_a_log"][i][2 * jq:2 * jq + 2]
    pv[0:2, 31] = p["gdn_dt_bias"][i][2 * jq:2 * jq + 2]
    pv[:, 32] = p["gdn_norm"][i]
    mu = p["rwkv_mu"][i]
    for t_ in range(2):
        pv[:, 33 + t_] = mu[0 + jq * 256 + t_ * 128 + pp]
        pv[:, 35 + t_] = mu[1024 + jq * 256 + t_ * 128 + pp]
        pv[:, 37 + t_] = mu[2048 + jq * 256 + t_ * 128 + pp]
    pv[:64, 39] = mu[3072:3136]
    pv[:64, 40] = mu[3136:3200]
    pv[:, 41] = mu[3200:3328]
    pv[:32, 42] = mu[3328:3360]
    for t_ in range(2):
        sl = jq * 256 + t_ * 128 + pp
        pv[:, 43 + t_] = p["rwkv_w0"][i][sl]
        pv[:, 45 + t_] = p["rwkv_a0"][i][sl]
        pv[:, 47 + t_] = p["rwkv_kk"][i][sl]
        pv[:, 49 + t_] = p["rwkv_ka"][i][sl]
        pv[:, 51 + t_] = p["rwkv_rk"][i].reshape(-1)[sl]
        pv[:, 53 + t_] = p["rwkv_ln_w"][i][sl]
        pv[:, 55 + t_] = p["rwkv_ln_b"][i][sl]
    rg2 = np.zeros((256, 256), np.float32)
    rg2[:160] = p["rwkv_g2"][i][:, jq * 256:(jq + 1) * 256]
    gpre = np.ascontiguousarray(p["pre_mix"][i].reshape(D // 128, 128).T)
    return dict(win=tile_w(W), pv=pv, gpre=gpre.astype(np.float32),
                gw2=np.ascontiguousarray(p["gla_w2"][i][:, jq * 128:(jq + 1) * 128]),
                rw2=np.ascontiguousarray(p["rwkv_w2"][i][:, jq * 256:(jq + 1) * 256]),
                ra2=np.ascontiguousarray(p["rwkv_a2"][i][:, jq * 256:(jq + 1) * 256]), rg2=rg2)


def const_tables(T):
    half = 128
    inv_freq = (10000.0 ** (-np.arange(half, dtype=np.float32) / half)).astype(np.float32)
    ang = (np.arange(T, dtype=np.float32)[None, :] * inv_freq[:, None]).astype(np.float32)
    rope = np.stack([np.cos(ang), np.sin(ang)], 0).astype(np.float32)
    bones = np.zeros((128, 128), np.float32)
    bones[:64, :64] = 1.0
    bones[64:, 64:] = 1.0
    consts = np.stack([np.eye(128, dtype=np.float32), np.ones((128, 128), np.float32), bones], 0)
    return rope, consts


def run_mixer(i, hT, p):
    B, _, T = hT.shape
    rope, consts = const_tables(T)
    in_maps = []
    for c in range(NCORES):
        b, jq = divmod(c, 4)
        m = mixer_params(i, jq, T, p)
        m["xT"] = np.ascontiguousarray(hT[b])
        m["rope"] = rope
        m["consts"] = consts
        in_maps.append(m)
    nc = build_mixer(T)
    res = run_bass_kernel_spmd(nc, in_maps, core_ids=list(range(NCORES)), **({"trace": True} if TRACE else {}))
    LAST_NS.append(res.exec_time_ns)
    oT = np.zeros((B, D, T), np.float32)
    for c in range(NCORES):
        b, jq = divmod(c, 4)
        o = res.results[c]["outT"]
        for m_ in range(4):
            oT[b, m_ * 1024 + jq * 256:m_ * 1024 + (jq + 1) * 256] = o[m_ * 256:(m_ + 1) * 256]
    return oT


def kernel(**inp):
    inp = {k: np.asarray(v) for k, v in inp.items()}
    x = inp["x"].astype(np.float32)
    B, S, _ = x.shape
    meta = np.broadcast_to(inp["meta"][None], (B, NMETA, D))
    h = np.concatenate([meta, x], axis=1)
    hT = np.ascontiguousarray(h.transpose(0, 2, 1))
    for i in range(2):
        oT = run_mixer(i, hT, inp)
        wts = dense_weights(i, inp["w_branch"], inp["w_gate"], inp["w_out"], inp["w_up"], inp["w_down"],
                            inp["ffn_conv"], inp["pre_mix"], inp["post_mix"], inp["pre_ffn"], inp["post_ffn"])
        hT = run_dense(hT, oT, wts)
    return np.ascontiguousarray(hT[:, :, NMETA:].transpose(0, 2, 1)).astype(np.float32)


def rstd_from_ss(P, rstd, ss_ps, n, nt, eps=EPS):
    if not hasattr(P, "epsc"):
        P.epsc = make_eps(P)
    rsqrt(P, rstd, rstd[:, :nt], ss_ps, ss_ps[:, :nt], 1.0 / n, P.epsc[:, 0:1], P.epsc)
```
